# Optimizing a Trainium2 kernel written in Bass

```python
import math
import jax, jax.numpy as jnp
from jax import lax
import numpy as np

D_MODEL = 1024
BATCH = 8
SEQ = 4096
DEPTH = 2
DEC_BATCH = 16
DEC_SEQ = 32
PAST_LEN = 4096

CHUNK = 64
POOL_WIDTH = D_MODEL // 4
POOL_WINDOWS = (2, 4, 8, 16)
POOL_GROUPS = len(POOL_WINDOWS)
POOL_GROUP_DIM = POOL_WIDTH // POOL_GROUPS
POOL_STATE = max(POOL_WINDOWS) - 1
CONV_DIM = D_MODEL // 4
CONV_K = 3
QK_DIM = 64
V_DIM = 2 * QK_DIM
ATTN_WIDTH = D_MODEL // 2
ATTN_HEADS = ATTN_WIDTH // V_DIM
QK_COLS = ATTN_HEADS * 2 * QK_DIM
MIX_WIDTH = POOL_WIDTH + CONV_DIM + ATTN_WIDTH
SPLIT_SIZES = (POOL_WIDTH, CONV_DIM, CONV_DIM, CONV_DIM, QK_COLS, QK_COLS, ATTN_WIDTH)
IN_COLS = sum(SPLIT_SIZES)
Q_BLOCK = 128
NUM_BUCKETS = 32
MAX_DISTANCE = 128
D_FF = 2816
ALPHA = (2 * DEPTH) ** 0.25
BETA = (8 * DEPTH) ** -0.25
LN_EPS = 1e-5
RMS_EPS = 1e-5
NEG_INF = -1e30

kernel_name = "hybrid_pool_conv_diffattn_streaming_encoder_step"


def layer_norm(x, g, b):
    xf = x.astype(jnp.float32)
    mu = jnp.mean(xf, axis=-1, keepdims=True)
    var = jnp.mean(jnp.square(xf - mu), axis=-1, keepdims=True)
    return ((xf - mu) * lax.rsqrt(var + LN_EPS) * g.astype(jnp.float32) + b.astype(jnp.float32)).astype(x.dtype)


def swiglu(x, w_in, w_out):
    gate, up = jnp.split(x @ w_in, 2, axis=-1)
    return (jax.nn.silu(gate) * up) @ w_out


def pool_mixer(u, left, pos, pool_w, pool_scale):
    b, s = u.shape[:2]
    ext = jnp.concatenate([left.astype(u.dtype), u], axis=1)
    cs = jnp.cumsum(ext.astype(jnp.float32), axis=1)
    cs = jnp.pad(cs, ((0, 0), (1, 0), (0, 0)))
    end = POOL_STATE + 1
    means = []
    for g, w in enumerate(POOL_WINDOWS):
        sl = slice(g * POOL_GROUP_DIM, (g + 1) * POOL_GROUP_DIM)
        win_sum = cs[:, end:end + s, sl] - cs[:, end - w:end - w + s, sl]
        count = jnp.minimum(w, pos + 1).astype(jnp.float32)[None, :, None]
        means.append(win_sum / count)
    mean = jnp.concatenate(means, axis=-1)
    d = (mean - u.astype(jnp.float32)).astype(u.dtype).reshape(b, s, POOL_GROUPS, POOL_GROUP_DIM)
    y = jnp.einsum("bsgc,gcd->bsgd", d, pool_w).reshape(b, s, POOL_WIDTH) * pool_scale
    return y, ext[:, -POOL_STATE:]


def short_conv(b_gate, c_gate, h, left, conv_w):
    z = c_gate * h
    s = z.shape[1]
    ext = jnp.concatenate([left.astype(z.dtype), z], axis=1)
    y = sum(conv_w[j] * ext[:, j:j + s] for j in range(CONV_K))
    return b_gate * y, ext[:, -(CONV_K - 1):]


def t5_bucket(rel):
    nb = NUM_BUCKETS // 2
    max_exact = nb // 2
    ret = (rel > 0).astype(jnp.int32) * nb
    n = jnp.abs(rel)
    nf = jnp.maximum(n, 1).astype(jnp.float32)
    large = max_exact + (jnp.log(nf / max_exact) / math.log(MAX_DISTANCE / max_exact) * (nb - max_exact)).astype(jnp.int32)
    large = jnp.minimum(large, nb - 1)
    return ret + jnp.where(n < max_exact, n, large)


def diff_attn_block(q, k, v, q_pos, k_pos, rel_bias, lam):
    bias = jnp.transpose(rel_bias[t5_bucket(k_pos[None, :] - q_pos[:, None])], (2, 0, 1)).astype(jnp.float32)
    mask = (k_pos[None, :] // CHUNK) <= (q_pos[:, None] // CHUNK)
    s = jnp.einsum("bqhcd,bkhcd->bchqk", q.astype(jnp.float32), k.astype(jnp.float32)) * (QK_DIM ** -0.5) + bias
    s = jnp.where(mask, s, NEG_INF)
    p = jax.nn.softmax(s, axis=-1)
    w = p[:, 0] - lam * p[:, 1]
    return jnp.einsum("bhqk,bkhd->bqhd", w, v.astype(jnp.float32))


def diff_attention(q, k, v, q_pos, k_pos, rel_bias, lam, lam_init, subln_g, sweep):
    b, s = q.shape[:2]
    if sweep:
        nblk = s // Q_BLOCK
        qb = jnp.moveaxis(q.reshape(b, nblk, Q_BLOCK, ATTN_HEADS, 2, QK_DIM), 1, 0)
        pb = q_pos.reshape(nblk, Q_BLOCK)
        o = lax.map(lambda a: diff_attn_block(a[0], k, v, a[1], k_pos, rel_bias, lam), (qb, pb))
        o = jnp.moveaxis(o, 0, 1).reshape(b, s, ATTN_HEADS, V_DIM)
    else:
        o = diff_attn_block(q, k, v, q_pos, k_pos, rel_bias, lam)
    o = o * lax.rsqrt(jnp.mean(jnp.square(o), axis=-1, keepdims=True) + RMS_EPS) * subln_g.astype(jnp.float32)
    o = o * (1.0 - lam_init)
    return o.reshape(b, s, ATTN_WIDTH).astype(q.dtype)


def encoder_layer(x, pos, pool_left, conv_left, past_k, past_v, lam_init,
                  ln_g, ln_b, w_ffn_in, w_ffn_out, w_in, w_out,
                  pool_w, pool_scale, conv_w, diff_lambda, subln_g, rel_bias):
    b, s = x.shape[:2]
    x = layer_norm(ALPHA * x + 0.5 * swiglu(x, w_ffn_in[0], w_ffn_out[0]), ln_g[0], ln_b[0])
    proj = x @ w_in
    parts = []
    off = 0
    for n in SPLIT_SIZES:
        parts.append(proj[..., off:off + n])
        off += n
    u_pool, b_gate, c_gate, h_conv, q, k, v = parts
    q = q.reshape(b, s, ATTN_HEADS, 2, QK_DIM)
    k = k.reshape(b, s, ATTN_HEADS, 2, QK_DIM)
    v = v.reshape(b, s, ATTN_HEADS, V_DIM)
    if past_k is None:
        kk, vv, k_pos, sweep = k, v, pos, True
    else:
        p_len = past_k.shape[1]
        kk = jnp.concatenate([past_k.reshape(b, p_len, ATTN_HEADS, 2, QK_DIM).astype(k.dtype), k], axis=1)
        vv = jnp.concatenate([past_v.astype(v.dtype), v], axis=1)
        k_pos = jnp.concatenate([jnp.arange(p_len, dtype=jnp.int32), pos])
        sweep = False
    dl = diff_lambda.astype(jnp.float32)
    lam = jnp.exp(jnp.sum(dl[0] * dl[1])) - jnp.exp(jnp.sum(dl[2] * dl[3])) + lam_init
    attn = diff_attention(q, kk, vv, pos, k_pos, rel_bias, lam, lam_init, subln_g, sweep)
    pool_out, pool_state = pool_mixer(u_pool, pool_left, pos, pool_w, pool_scale)
    conv_out, conv_state = short_conv(b_gate, c_gate, h_conv, conv_left, conv_w)
    mix = jnp.concatenate([pool_out, conv_out, attn], axis=-1) @ w_out
    x = layer_norm(ALPHA * x + mix, ln_g[1], ln_b[1])
    x = layer_norm(ALPHA * x + 0.5 * swiglu(x, w_ffn_in[1], w_ffn_out[1]), ln_g[2], ln_b[2])
    return x, k.reshape(b, s, ATTN_HEADS, 2 * QK_DIM), v, pool_state, conv_state


def setup_inputs(seed: int = 0) -> dict:
    key = jax.random.key(seed)
    ks = jax.random.split(key, 20)
    nrm = jax.random.normal
    f32 = jnp.float32
    return {
        "x_prompt": nrm(ks[0], (BATCH, SEQ, D_MODEL), f32),
        "x_sample": nrm(ks[1], (DEC_BATCH, DEC_SEQ, D_MODEL), f32),
        "cache_k": nrm(ks[2], (DEPTH, DEC_BATCH, PAST_LEN, ATTN_HEADS, 2 * QK_DIM), f32),
        "cache_v": nrm(ks[3], (DEPTH, DEC_BATCH, PAST_LEN, ATTN_HEADS, V_DIM), f32),
        "state_pool": nrm(ks[4], (DEPTH, DEC_BATCH, POOL_STATE, POOL_WIDTH), f32),
        "state_conv": nrm(ks[5], (DEPTH, DEC_BATCH, CONV_K - 1, CONV_DIM), f32),
        "ln_g": 1.0 + 0.05 * nrm(ks[6], (DEPTH, 3, D_MODEL), f32),
        "ln_b": 0.02 * nrm(ks[7], (DEPTH, 3, D_MODEL), f32),
        "w_ffn_in": nrm(ks[8], (DEPTH, 2, D_MODEL, 2 * D_FF), f32) * D_MODEL ** -0.5,
        "w_ffn_out": nrm(ks[9], (DEPTH, 2, D_FF, D_MODEL), f32) * (D_FF ** -0.5 * BETA),
        "w_in": nrm(ks[10], (DEPTH, D_MODEL, IN_COLS), f32) * D_MODEL ** -0.5,
        "w_out": nrm(ks[11], (DEPTH, MIX_WIDTH, D_MODEL), f32) * (MIX_WIDTH ** -0.5 * BETA),
        "pool_w": nrm(ks[12], (DEPTH, POOL_GROUPS, POOL_GROUP_DIM, POOL_GROUP_DIM), f32) * POOL_GROUP_DIM ** -0.5,
        "pool_scale": 1.0 + 0.1 * nrm(ks[13], (DEPTH, POOL_WIDTH), f32),
        "conv_w": nrm(ks[14], (DEPTH, CONV_K, CONV_DIM), f32) * CONV_K ** -0.5,
        "diff_lambda": 0.1 * nrm(ks[15], (DEPTH, 4, QK_DIM), f32),
        "subln_g": 1.0 + 0.05 * nrm(ks[16], (DEPTH, V_DIM), f32),
        "rel_bias": 0.5 * nrm(ks[17], (NUM_BUCKETS, ATTN_HEADS), f32),
    }


def reference(x_prompt, x_sample, cache_k, cache_v, state_pool, state_conv,
              ln_g, ln_b, w_ffn_in, w_ffn_out, w_in, w_out,
              pool_w, pool_scale, conv_w, diff_lambda, subln_g, rel_bias):
    bp, sp = x_prompt.shape[:2]
    bs, ss = x_sample.shape[:2]
    past = cache_k.shape[2]
    pos_p = jnp.arange(sp, dtype=jnp.int32)
    pos_s = past + jnp.arange(ss, dtype=jnp.int32)
    hp, hs = x_prompt, x_sample
    kp_l, vp_l, plp_l, cvp_l = [], [], [], []
    ks_l, vs_l, pls_l, cvs_l = [], [], [], []
    for l in range(DEPTH):
        lam_init = 0.8 - 0.6 * math.exp(-0.3 * l)
        lw = (ln_g[l], ln_b[l], w_ffn_in[l], w_ffn_out[l], w_in[l], w_out[l],
              pool_w[l], pool_scale[l], conv_w[l], diff_lambda[l], subln_g[l], rel_bias)
        hp, kp, vp, plp, cvp = encoder_layer(
            hp, pos_p, jnp.zeros((bp, POOL_STATE, POOL_WIDTH), hp.dtype),
            jnp.zeros((bp, CONV_K - 1, CONV_DIM), hp.dtype), None, None, lam_init, *lw)
        hs, k_s, v_s, pls, cvs = encoder_layer(
            hs, pos_s, state_pool[l], state_conv[l], cache_k[l], cache_v[l], lam_init, *lw)
        kp_l.append(kp); vp_l.append(vp); plp_l.append(plp); cvp_l.append(cvp)
        ks_l.append(k_s); vs_l.append(v_s); pls_l.append(pls); cvs_l.append(cvs)
    new_k_prompt = jnp.stack(kp_l)
    new_v_prompt = jnp.stack(vp_l)
    new_pool_prompt = jnp.stack(plp_l)
    new_conv_prompt = jnp.stack(cvp_l)
    new_k_sample = jnp.stack(ks_l)
    new_v_sample = jnp.stack(vs_l)
    new_pool_sample = jnp.stack(pls_l)
    new_conv_sample = jnp.stack(cvs_l)
    return (hp, hs, new_k_prompt, new_v_prompt, new_pool_prompt, new_conv_prompt,
            new_k_sample, new_v_sample, new_pool_sample, new_conv_sample)
```

```python
import math
import os
from contextlib import ExitStack

import numpy as np
import concourse.bass as bass
import concourse.mybir as mybir
from concourse.bass_utils import run_bass_kernel_spmd

F32 = mybir.dt.float32
BF16 = mybir.dt.bfloat16
AF = mybir.ActivationFunctionType
ALU = mybir.AluOpType

NCORES = 8
D = 1024
DFF = 2816
NJ = 22
NH = 4
DEPTH = 2
SS = 32
SBATCH = 2
TS = SS * SBATCH
ALPHA = (2 * DEPTH) ** 0.25
LN_EPS2 = 1e-5 / (ALPHA * ALPHA)
RMS_EPS = 1e-5
ND = 1152
DOFF = 639
NEGM = -30000.0
OC_COLS = [0, 128, 512, 768, 256, 640, 896, 384, 1024, 1152, 1280, 1408, 1536, 1664, 1792, 1920]
OC_KIND = [("u", 0), ("u", 1), ("C", 0), ("h", 0), ("B", 0), ("C", 1), ("h", 1), ("B", 1),
           ("q", 0), ("q", 1), ("q", 2), ("q", 3), ("k", 0), ("k", 1), ("k", 2), ("k", 3)]
POOL_W = (2, 4, 8, 16)
EPOCH = 20000


class Op:
    __slots__ = ("eng", "fn", "deps", "signal", "dma_key", "sig", "idx", "eidx", "group")

    def __init__(self, eng, fn, dma_key):
        self.eng = eng
        self.fn = fn
        self.deps = set()
        self.signal = False
        self.dma_key = dma_key
        self.sig = None
        self.idx = 0
        self.group = False


class Sched:
    ENGS = ("pe", "act", "dve", "pool", "sp")

    def __init__(self, nc):
        self.nc = nc
        self.ops = []
        self.last_writer = {}
        self.readers = {}
        self.eng_count = {}

    def add(self, eng, fn, reads=(), writes=(), dma_key=None, group=False):
        op = Op(eng, fn, dma_key)
        op.group = group
        op.idx = len(self.ops)
        deps = op.deps
        lw = self.last_writer
        rd = self.readers
        eidx = self.eng_count.get(eng, 0)
        op.eidx = eidx
        self.eng_count[eng] = eidx + 1
        is_dma = dma_key is not None

        def consider(p, raw):
            if p is op:
                return
            if p.dma_key is None and not is_dma and p.eng == eng:
                if eng == "pe" or not raw or eidx - p.eidx > 2:
                    return
            deps.add(p)

        for r in reads:
            w = lw.get(r)
            if w is not None:
                consider(w, True)
        for w_ in writes:
            w = lw.get(w_)
            if w is not None:
                consider(w, False)
            for x in rd.get(w_, ()):
                consider(x, False)
        for d in deps:
            d.signal = True
        for r in reads:
            rd.setdefault(r, []).append(op)
        for w_ in writes:
            lw[w_] = op
            rd[w_] = []
        self.ops.append(op)
        return op

    def dma(self, out, in_, reads, writes, key, eng="sp", group=False, **kw):
        return self.add(eng, lambda e: e.dma_start(out=out, in_=in_, **kw), reads, writes,
                        dma_key=key, group=group)

    def emit(self):
        nc = self.nc
        ops = self.ops
        cnt = {e: 0 for e in self.ENGS}
        dcnt = {}
        for op in ops:
            if op.dma_key is not None:
                dcnt[op.dma_key] = dcnt.get(op.dma_key, 0) + 1
                op.sig = ["d", op.dma_key, 16 * dcnt[op.dma_key]]
            elif op.signal:
                c = cnt[op.eng]
                op.sig = ["e", (op.eng, c // EPOCH), (c % EPOCH) + 1]
                cnt[op.eng] = c + 1
        for op in ops:
            if op.dma_key is not None and op.group:
                op.sig[2] = 16 * dcnt[op.dma_key]
                for d in op.deps:
                    assert d.dma_key != op.dma_key, "group DMA depends on its own group: %s" % op.dma_key
        sem_names = sorted({op.sig[1] for op in ops if op.sig is not None}, key=str)
        self.nsems = len(sem_names)
        with ExitStack() as st:
            sems = {}
            for i, k in enumerate(sem_names):
                sems[k] = st.enter_context(nc.semaphore("s%d" % i))
            block = st.enter_context(nc.Block())
            by_eng = {e: [op for op in ops if op.eng == e] for e in self.ENGS}
            final_dma = dict((k, 16 * v) for k, v in dcnt.items())

            def run_engine(e, lst, is_last_waiter):
                waited = {}
                eng_epoch = {}
                for op in lst:
                    best = {}
                    for d in op.deps:
                        kk = (d.sig[0], d.sig[1])
                        if kk not in best or d.sig[2] > best[kk].sig[2]:
                            best[kk] = d
                    for d in sorted(best.values(), key=lambda o: o.idx):
                        kind, key, val = d.sig
                        if kind == "e":
                            pe_, ep = key
                            if eng_epoch.get(pe_, -1) > ep:
                                continue
                            if waited.get(key, 0) >= val:
                                continue
                            waited[key] = val
                            eng_epoch[pe_] = max(eng_epoch.get(pe_, -1), ep)
                        else:
                            if waited.get(key, 0) >= val:
                                continue
                            waited[key] = val
                        e.wait_ge(sems[key], val)
                    ins = op.fn(e)
                    if op.sig is not None:
                        ins.then_inc(sems[op.sig[1]], 16 if op.sig[0] == "d" else 1)
                if is_last_waiter:
                    for k, v in final_dma.items():
                        if waited.get(k, 0) < v:
                            e.wait_ge(sems[k], v)

            block.sync(lambda e: run_engine(e, by_eng["sp"], True))
            if by_eng["pe"]:
                block.tensor(lambda e: run_engine(e, by_eng["pe"], False))
            if by_eng["act"]:
                block.scalar(lambda e: run_engine(e, by_eng["act"], False))
            if by_eng["dve"]:
                block.vector(lambda e: run_engine(e, by_eng["dve"], False))
            if by_eng["pool"]:
                block.gpsimd(lambda e: run_engine(e, by_eng["pool"], False))


def _t5_bucket_np(rel):
    rel = np.asarray(rel, np.int64)
    nb, max_exact = 16, 8
    ret = (rel > 0).astype(np.int64) * nb
    n = np.abs(rel)
    nf = np.maximum(n, 1).astype(np.float32)
    lg = np.log(nf / np.float32(max_exact)).astype(np.float32)
    v = (lg / np.float32(math.log(128 / 8))).astype(np.float32) * np.float32(nb - max_exact)
    large = np.minimum(max_exact + v.astype(np.int32), nb - 1)
    return ret + np.where(n < max_exact, n, large)


def _host_consts():
    d = np.arange(ND) - DOFF
    bk = _t5_bucket_np(d)
    oh = np.zeros((32, ND), np.float32)
    oh[bk, np.arange(ND)] = 1.0
    rc = np.zeros((128, 32), np.float32)
    for i in range(2):
        for p in range(128):
            w = POOL_W[2 * i + (1 if p >= 64 else 0)]
            for t in range(16):
                rc[p, i * 16 + t] = 1.0 / min(w, t + 1)
    ident = np.eye(128, dtype=np.float32)
    return oh, rc, ident


class Tile:
    def __init__(self, kind, idx, T, tok0, segs):
        self.kind = kind
        self.idx = idx
        self.T = T
        self.tok0 = tok0
        self.segs = segs


class _DummySched:
    def __init__(self):
        self.ops = []

    def add(self, *a, **k):
        return None

    def dma(self, *a, **k):
        return None


def build_program(SEQ, PAST, stop=None, wplan=None):
    plan_only = wplan is None
    if plan_only:
        my_plan = []
    NT = SEQ // 512
    NG = PAST // 512
    NKT = SEQ // 128
    nc = bass.Bass("TRN2", target_bir_lowering=False)
    S = _DummySched() if plan_only else Sched(nc)

    def din(name, shape, dt=F32):
        return nc.dram_tensor(name, list(shape), dt, kind="ExternalInput").ap()

    def dout(name, shape):
        return nc.dram_tensor(name, list(shape), F32, kind="ExternalOutput").ap()

    def dscr(name, shape, dt):
        return nc.dram_tensor(name, list(shape), dt, kind="Internal").ap()

    def sb(name, shape, dt):
        return nc.alloc_sbuf_tensor(name, list(shape), dt).ap()

    xT = din("xT", [D, SEQ])
    xTs = din("xTs", [D, TS])
    ckT = din("ckT", [2, SBATCH, NH, 128, PAST])
    cvt = din("cvt", [2, SBATCH, NH, NG, 128, 512])
    stp = din("stp", [2, SBATCH, 2, 128, 15])
    stc = din("stc", [2, SBATCH, 2, 128, 2])
    lng_d = din("lng", [128, 48])
    lnb_d = din("lnb", [128, 48])
    wfi = din("wfi", [2, 2, NJ, 128, 2048])
    wfo = din("wfo", [2, 2, 2, 128, NJ * 512])
    wina = din("wina", [2, 8, 128, 2048])
    winv = din("winv", [2, 128, 4096])
    wout = din("wout", [2, 4, 128, 2048])
    pwbd_d = din("pwbd", [128, 4 * 128])
    pscale_d = din("pscale", [128, 4])
    convw_d = din("convw", [128, 12])
    dlam_d = din("dlam", [1, 512])
    subln_d = din("sublng", [128, 2])
    relb_d = din("relb", [32, 4])
    oh_d = din("ohc", [32, ND])
    rc_d = din("rcfix", [128, 32])
    ident_d = din("identc", [128, 128])

    oyT = dout("oyT", [D, SEQ])
    oyTs = dout("oyTs", [D, TS])
    okT = dout("okT", [2, 512, SEQ])
    ov = dout("ov", [2, SEQ, 512])
    opool = dout("opool", [2, 2, 128, 15])
    oconv = dout("oconv", [2, 2, 128, 2])
    okTs = dout("okTs", [2, 512, TS])
    ovs = dout("ovs", [2, TS, 512])
    opools = dout("opools", [2, SBATCH, 2, 128, 15])
    oconvs = dout("oconvs", [2, SBATCH, 2, 128, 2])

    wfi_s = dscr("wfi_s", [2, 2, NJ, 128, 2048], BF16)
    wfo_s = dscr("wfo_s", [2, 2, 2, 128, NJ * 512], BF16)
    wina_s = dscr("wina_s", [2, 8, 128, 2048], BF16)
    winv_s = dscr("winv_s", [2, 128, 4096], BF16)
    wout_s = dscr("wout_s", [2, 4, 128, 2048], BF16)
    ckT_s = dscr("ckT_s", [2, SBATCH, NH, 128, PAST], BF16)
    cvt_s = dscr("cvt_s", [2, SBATCH, NH, NG, 128, 512], BF16)
    xs_p = dscr("xs_p", [D, SEQ], F32)
    xs_s = dscr("xs_s", [D, TS], F32)
    fvd = dscr("fvd", [4, ND], F32)

    KT = sb("KT", [128, NH, SEQ], BF16)
    VV = sb("VV", [128, NKT, 512], BF16)
    xres = sb("xres", [128, 8, 512], F32)
    xbf = sb("xbf", [128, 8, 512], BF16)
    zsq = sb("zsq", [128, 8, 512], BF16)
    Hb = sb("Hb", [128, NJ * 512], BF16)
    QT = sb("QT", [128, NH, 512], BF16)
    mixT = sb("mixT", [128, 8, 512], BF16)
    BT = sb("BT", [128, 20, 512], BF16)
    NWR = 4
    WR = [sb("WR%d" % i, [128, 2048], BF16) for i in range(NWR)]
    ubuf = sb("ubuf", [128, 2, 528], F32)
    Bbuf = sb("Bbuf", [128, 2, 512], F32)
    Cbuf = sb("Cbuf", [128, 2, 512], F32)
    zbuf = sb("zbuf", [128, 2, 516], F32)
    pa = sb("pa", [128, 528], F32)
    pb = sb("pb", [128, 528], F32)
    dpool = sb("dpool", [128, 2, 512], BF16)
    cacc = sb("cacc", [128, 512], F32)
    kst = [sb("kst%d" % i, [128, 512], F32) for i in range(2)]
    vst = [sb("vst%d" % i, [128, 512], F32) for i in range(2)]
    KTs = sb("KTs", [128, NH, TS], BF16)
    Vs = sb("Vs", [32, SBATCH, 512], BF16)
    ones_ln = sb("ones_ln", [128, 128], BF16)
    ones1 = sb("ones1", [128, 128], BF16)
    ones_rms = sb("ones_rms", [128, 128], BF16)
    ident = sb("ident", [128, 128], BF16)
    PW = sb("PW", [128, 4 * 128], BF16)
    lng = sb("lng_sb", [128, 48], F32)
    lnb = sb("lnb_sb", [128, 48], F32)
    pscale = sb("pscale_sb", [128, 4], F32)
    convw = sb("convw_sb", [128, 12], F32)
    subln = sb("subln_sb", [128, 2], F32)
    rcfix = sb("rcfix_sb", [128, 32], F32)
    lamneg = sb("lamneg", [128, 2], F32)
    smalls = sb("smalls", [128, 16], F32)
    dlb = sb("dlb", [128, 512], F32)
    tmpf = sb("tmpf", [128, 16], F32)

    ZT = sb("ZT", [128, 512], BF16)
    PSA = nc.alloc_psum_tensor("psall", [128, 8, 512], F32).ap()
    PS = [PSA[:, i, :] for i in range(8)]

    def Hc(j):
        return Hb[:, j * 512:(j + 1) * 512]

    def Hf(j):
        return Hb[:, j * 512:(j + 2) * 512].bitcast(F32)

    def Hr(j, n=1):
        return ["H%d" % (j + i) for i in range(n)]

    PT = [Hb[:, (2 * i) * 512:(2 * i + 2) * 512].rearrange("p (c t) -> p c t", c=2) for i in range(3)]
    PTr = [Hr(2 * i, 2) for i in range(3)]
    R0, R0r = Hf(6), Hr(6, 2)
    R1, R1r = Hf(8), Hr(8, 2)
    O0, O0r = Hf(10), Hr(10, 2)
    O1, O1r = Hf(12), Hr(12, 2)
    RS, RSr = Hf(14), Hr(14, 2)
    SQ, SQr = Hc(16), Hr(16, 1)
    LT0, LT0r = Hf(16), Hr(16, 2)
    LT1, LT1r = Hf(18), Hr(18, 2)
    LT2, LT2r = Hf(20), Hr(20, 2)
    CR = [(Hc(18), Hc(19), Hr(18, 2)), (Hc(20), Hc(21), Hr(20, 2))]

    cast_pieces = {}

    def wfi_group(j):
        p = j // 2
        return 0 if p < 1 else (1 if p < 5 else 2)

    cast_n = [0]

    deferred = []
    defer_mode = [False]

    def cast(dst, src, res, key=None):
        lst = cast_pieces.setdefault(res, [])
        nm = "%s.%d" % (res, len(lst))
        lst.append(nm)

        def emit(extra_reads=()):
            k = cast_n[0] % 2
            cast_n[0] += 1
            S.dma(dst, src, list(extra_reads), [nm, "cwslot%d" % k], "cw%d" % k, eng="pool")

        if defer_mode[0]:
            deferred.append(emit)
        else:
            emit()

    def cast_weights(l):
        for f in range(2):
            if f == 1:
                for i in range(4):
                    cast(wina_s[l, 2 * i:2 * i + 2], wina[l, 2 * i:2 * i + 2], "wina%d" % l)
                cast(winv_s[l], winv[l], "winv%d" % l)
                for i in range(2):
                    cast(wout_s[l, 2 * i:2 * i + 2], wout[l, 2 * i:2 * i + 2], "wout%d" % l)
            for i in range(NJ // 2):
                g = wfi_group(2 * i)
                cast(wfi_s[l, f, 2 * i:2 * i + 2], wfi[l, f, 2 * i:2 * i + 2], "wfi%d%d.%d" % (l, f, g))
            for hf in range(2):
                for i in range(2):
                    cast(wfo_s[l, f, hf][:, i * 5632:(i + 1) * 5632], wfo[l, f, hf][:, i * 5632:(i + 1) * 5632], "wfo%d%d" % (l, f))

    def cast_caches(l):
        for b in range(SBATCH):
            for h in range(NH):
                cast(ckT_s[l, b, h], ckT[l, b, h], "ckT%d%d" % (l, b))
                g0 = 0
                while g0 < NG:
                    g1 = min(NG, g0 + 4)
                    cast(cvt_s[l, b, h, g0:g1], cvt[l, b, h, g0:g1], "cvt%d%d" % (l, b))
                    g0 = g1

    def setup():
        for dst, src, nm in ((lng, lng_d, "lng"), (lnb, lnb_d, "lnb"), (pscale, pscale_d, "pscale"),
                             (convw, convw_d, "convw"), (subln, subln_d, "subln"), (rcfix, rc_d, "rcfix")):
            S.dma(dst, src, [], [nm], "setup", group=True)
        _skip = os.environ.get("K_SKIP", "").split(",")
        if "lam" not in _skip:
            S.dma(dlb, bass.AP(dlam_d.tensor, 0, [[0, 128], [1, 512]]), [], ["dlb"], "setup", group=True)
        S.add("dve", lambda e: e.memset(ZT, 0.0), [], ["ZT"])
        S.add("dve", lambda e: e.memset(ones_ln, 1.0 / D), [], ["ones_ln"])
        S.add("dve", lambda e: e.memset(ones1, 1.0), [], ["ones1"])
        S.add("dve", lambda e: e.memset(ones_rms, 1.0 / 128.0), [], ["ones_rms"])
        for l in range(2):
            for k in range(2):
                a = dlb[:, (l * 4 + 2 * k) * 64:(l * 4 + 2 * k + 1) * 64]
                b_ = dlb[:, (l * 4 + 2 * k + 1) * 64:(l * 4 + 2 * k + 2) * 64]
                col = smalls[:, l * 4 + k:l * 4 + k + 1]
                S.add("dve", lambda e, a=a, b_=b_: e.tensor_tensor(pa[:, 0:64], a, b_, ALU.mult),
                      ["dlb"], ["pa"])
                S.add("dve", lambda e, col=col: e.reduce_sum(col, pa[:, 0:64], mybir.AxisListType.X), ["pa"], ["smalls"])
            S.add("act", lambda e, l=l: e.activation(smalls[:, l * 4 + 2:l * 4 + 4], smalls[:, l * 4:l * 4 + 2], AF.Exp),
                  ["smalls"], ["smalls"])
            lam_init = 0.8 - 0.6 * math.exp(-0.3 * l)
            S.add("dve", lambda e, l=l: e.tensor_tensor(smalls[:, 8 + l:9 + l], smalls[:, l * 4 + 3:l * 4 + 4],
                                                        smalls[:, l * 4 + 2:l * 4 + 3], ALU.subtract),
                  ["smalls"], ["smalls"])
            S.add("dve", lambda e, l=l, li=lam_init: e.tensor_scalar(lamneg[:, l:l + 1], smalls[:, 8 + l:9 + l], -li, None, ALU.add),
                  ["smalls"], ["lamneg"])

    def setup_bt():
        pages = []

        def add_pages(buf2d_bf16, ncols_bf16):
            for k in range(ncols_bf16 // 1024):
                pages.append(buf2d_bf16[:, k * 1024:(k + 1) * 1024].bitcast(F32))

        add_pages(KT.rearrange("p h s -> p (h s)"), NH * SEQ)
        add_pages(VV.rearrange("p k c -> p (k c)"), NKT * 512)
        add_pages(mixT.rearrange("p c t -> p (c t)"), 8 * 512)
        add_pages(QT.rearrange("p h t -> p (h t)"), NH * 512)
        relb = pages[0][0:32, 0:4]
        ohs = [pages[1][0:32, :], pages[2][0:32, :], pages[3][0:32, 0:ND - 1024]]
        fvs = [pages[4][0:4, :], pages[5][0:4, :], pages[6][0:4, :]]
        stg = pages[7:27]
        NSTG = len(stg)
        S.dma(relb, relb_d, [], ["pg0"], "setupbt", group=True)
        for i in range(3):
            n = min(512, ND - i * 512)
            S.dma(ohs[i][:, 0:n], oh_d[:, i * 512:i * 512 + n], [], ["pg%d" % (1 + i)], "setupbt", group=True)
        for i in range(3):
            n = min(512, ND - i * 512)
            S.add("pe", lambda e, i=i, n=n: e.matmul(PS[i][0:4, 0:n], relb, ohs[i][:, 0:n], start=True, stop=True),
                  ["pg0", "pg%d" % (1 + i)], ["ps%d" % i])
        S.add("act", lambda e: e.copy(smalls[0:4, 12:13], PS[0][0:4, 0:1]), ["ps0"], ["smalls_c"])
        for i in range(3):
            n = min(512, ND - i * 512)
            S.add("dve", lambda e, i=i, n=n: e.tensor_scalar(fvs[i][:, 0:n], PS[i][0:4, 0:n], smalls[0:4, 12:13], None, ALU.subtract),
                  ["ps%d" % i, "smalls_c"], ["pg%d" % (4 + i)])
            S.dma(fvd[:, i * 512:i * 512 + n], fvs[i][:, 0:n], ["pg%d" % (4 + i)], ["fvd%d" % i], "fvd_w%d" % i)
        k = 0
        for h in range(NH):
            for mi in range(5):
                m = mi - 1
                base = DOFF + 128 * m - 511
                s_ = k % NSTG
                src = bass.AP(fvd.tensor, h * ND + base, [[1, 128], [1, 512]])
                S.dma(stg[s_], src, ["fvd0", "fvd1", "fvd2"], ["hkp%d" % s_], "hk%d" % s_)
                bt = BT[:, h * 5 + mi, :]
                S.add("dve", lambda e, bt=bt, s_=s_: e.tensor_copy(bt, stg[s_][:, ::-1]), ["hkp%d" % s_], ["BT"])
                if m >= 0:
                    if m > 0:
                        S.add("dve", lambda e, h=h, mi=mi, m=m: e.memset(BT[0:64, h * 5 + mi, 0:128 * m], NEGM), [], ["BT"])
                    S.add("dve", lambda e, h=h, mi=mi, m=m: e.memset(BT[64:128, h * 5 + mi, 0:128 * m + 64], NEGM), [], ["BT"])
                k += 1

    ring = [0]
    wl_emitted = [0]
    DEFER_START = 60
    DEFER_EVERY = 3 if SEQ >= 2048 else 1
    LOOK = NWR - 2

    def wsrc(tag):
        kind = tag[0]
        if kind == "wfi":
            _, l, f, j = tag
            return wfi_s[l, f, j], 2048, "wfi%d%d.%d" % (l, f, wfi_group(j))
        if kind == "wfo":
            _, l, f, hf, jg, nj = tag
            return wfo_s[l, f, hf][:, jg * 512:(jg + nj) * 512], nj * 512, "wfo%d%d" % (l, f)
        if kind == "wina":
            _, l, ocp = tag
            return wina_s[l, ocp], 2048, "wina%d" % l
        if kind == "winv":
            _, l, half = tag
            return winv_s[l][:, half * 2048:(half + 1) * 2048], 2048, "winv%d" % l
        if kind == "wout":
            _, l, cp = tag
            return wout_s[l, cp], 2048, "wout%d" % l
        raise ValueError(tag)

    def load_w(tag):
        k = ring[0]
        ring[0] += 1
        if plan_only:
            my_plan.append(tag)
            return WR[k % NWR], "WR%d" % (k % NWR)
        upto = min(len(wplan), k + LOOK + 1)
        while wl_emitted[0] < upto:
            i = wl_emitted[0]
            wl_emitted[0] += 1
            src, ncols, res = wsrc(wplan[i])
            sl = i % NWR
            S.dma(WR[sl][:, 0:ncols], src, cast_pieces[res], ["WR%d" % sl], "WR%d" % sl)
            if deferred and i >= DEFER_START and (i - DEFER_START) % DEFER_EVERY == 0:
                deferred.pop(0)(["WR%d" % sl])
        assert wplan[k] == tag, (wplan[k], tag)
        return WR[k % NWR], "WR%d" % (k % NWR)

    STG = Hb[:, 0:16 * 512].bitcast(F32).rearrange("p (c t) -> p c t", c=8)

    warm_next = [0]

    def dummy(n, bank):
        for _ in range(n):
            S.add("pe", lambda e: e.matmul(PS[bank][:, :], ones1, ZT, start=True, stop=True), ["ones1", "ZT"], ["ps%d" % bank])

    def warm_pad():
        if warm_next[0] > 0:
            warm_next[0] -= 1
            dummy(3, 7)

    next_x = [None]
    prefetched = set()

    def prefetch_x():
        if next_x[0] is None:
            return
        l2, t2 = next_x[0]
        T2 = t2.T
        for c in range(8):
            if t2.kind == "p":
                src = (xT if l2 == 0 else xs_p)[c * 128:(c + 1) * 128, t2.tok0:t2.tok0 + T2]
                rd = [] if l2 == 0 else ["xsp%d.%d" % (t2.idx, c)]
            else:
                src = (xTs if l2 == 0 else xs_s)[c * 128:(c + 1) * 128, :]
                rd = [] if l2 == 0 else ["xss.%d" % c]
            S.dma(xbf[:, c, :T2], src, rd, ["xbf%d" % c], "xp%d" % c, eng="pool")
        prefetched.add((l2, t2.kind, t2.idx))

    def ln_apply(l, lni, T, final=False):
        for c in range(8):
            S.add("pe", lambda e, c=c: e.matmul(PS[0][:, :T], ones_ln, xbf[:, c, :T], start=(c == 0), stop=(c == 7)),
                  ["xbf%d" % c, "ones_ln"], ["ps0"])
        for c in range(8):
            S.add("pe", lambda e, c=c: e.matmul(PS[1][:, :T], ones_ln, zsq[:, c, :T], start=(c == 0), stop=(c == 7)),
                  ["zsq%d" % c, "ones_ln"], ["ps1"])
        if final:
            prefetch_x()
        elif T == 512:
            dummy(22, 2)
            warm_next[0] = 8
        mean, m2, rstd = LT0[:, :T], LT1[:, :T], LT2[:, :T]
        S.add("act", lambda e: e.copy(mean, PS[0][:, :T]), ["ps0"], LT0r)
        S.add("act", lambda e: e.activation(m2, PS[0][:, :T], AF.Square), ["ps0"], LT1r)
        S.add("dve", lambda e: e.scalar_tensor_tensor(m2, PS[1][:, :T], LN_EPS2, m2, ALU.add, ALU.subtract), ["ps1"] + LT1r, LT1r)
        S.add("act", lambda e: e.activation(rstd, m2, AF.Ln), LT1r, LT2r)
        S.add("act", lambda e: e.activation(rstd, rstd, AF.Exp, scale=-0.5), LT2r, LT2r)
        for c in range(8):
            gi = (l * 3 + lni) * 8 + c
            xc = xres[:, c, :T]
            xr = "xres%d" % c
            if final:
                S.add("pool", lambda e, xc=xc: e.tensor_tensor(xc, xc, mean, ALU.subtract), [xr] + LT0r, [xr])
                S.add("pool", lambda e, xc=xc: e.tensor_tensor(xc, xc, rstd, ALU.mult), [xr] + LT2r, [xr])
                S.add("pool", lambda e, xc=xc, c=c, gi=gi: e.tensor_scalar(STG[:, c, :T], xc, lng[:, gi:gi + 1], lnb[:, gi:gi + 1], ALU.mult, ALU.add),
                      [xr, "lng", "lnb"], Hr(2 * c, 2))
                continue
            S.add("dve", lambda e, xc=xc: e.tensor_tensor(xc, xc, mean, ALU.subtract), [xr] + LT0r, [xr])
            S.add("dve", lambda e, xc=xc: e.tensor_tensor(xc, xc, rstd, ALU.mult), [xr] + LT2r, [xr])
            S.add("act", lambda e, xc=xc, c=c, gi=gi: e.activation(xbf[:, c, :T], xc, AF.Identity, scale=lng[:, gi:gi + 1], bias=lnb[:, gi:gi + 1]),
                  [xr, "lng", "lnb"], ["xbf%d" % c])
        for c in range(8):
            if final:
                break
            gi = (l * 3 + lni) * 8 + c
            xc = xres[:, c, :T]
            xr = "xres%d" % c
            if c < 4:
                S.add("act", lambda e, xc=xc, gi=gi: e.activation(xc, xc, AF.Identity, scale=lng[:, gi:gi + 1], bias=lnb[:, gi:gi + 1]),
                      [xr, "lng", "lnb"], [xr])
            else:
                S.add("dve", lambda e, xc=xc, gi=gi: e.tensor_scalar(xc, xc, lng[:, gi:gi + 1], lnb[:, gi:gi + 1], ALU.mult, ALU.add),
                      [xr, "lng", "lnb"], [xr])

    def residual_prep(c, bank, T, coef):
        xc = xres[:, c, :T]
        S.add("dve", lambda e: e.scalar_tensor_tensor(xc, PS[bank][:, :T], coef, xc, ALU.mult, ALU.add),
              ["ps%d" % bank, "xres%d" % c], ["xres%d" % c])
        S.add("dve", lambda e: e.tensor_copy(xbf[:, c, :T], xc), ["xres%d" % c], ["xbf%d" % c])
        S.add("act", lambda e: e.activation(zsq[:, c, :T], xc, AF.Square), ["xres%d" % c], ["zsq%d" % c])

    def ffn_ln(l, f, lni, T, final=False, mid_hook=None):
        wres = "wfi%d%d" % (l, f)
        for j in range(NJ):
            w, wr = load_w(("wfi", l, f, j))
            bg, bu = 2 * (j % 2), 2 * (j % 2) + 1
            for gu, bank in ((0, bg), (1, bu)):
                for kc in range(8):
                    S.add("pe", lambda e, w=w, gu=gu, kc=kc, bank=bank: e.matmul(
                        PS[bank][:, :T], w[:, (gu * 8 + kc) * 128:(gu * 8 + kc + 1) * 128], xbf[:, kc, :T],
                        start=(kc == 0), stop=(kc == 7)), [wr, "xbf%d" % kc], ["ps%d" % bank])
                    warm_pad()
            hj = Hc(j)[:, :T]
            S.add("act", lambda e, hj=hj, bg=bg: e.activation(hj, PS[bg][:, :T], AF.Silu), ["ps%d" % bg], Hr(j))
            S.add("dve", lambda e, hj=hj, bu=bu: e.tensor_tensor(hj, hj, PS[bu][:, :T], ALU.mult), ["ps%d" % bu] + Hr(j), Hr(j))
        if mid_hook is not None:
            mid_hook()
        wres = "wfo%d%d" % (l, f)
        for hf in range(2):
            for jg in range(0, NJ, 4):
                nj = min(4, NJ - jg)
                w, wr = load_w(("wfo", l, f, hf, jg, nj))
                for jj in range(nj):
                    j = jg + jj
                    for cc in range(4):
                        S.add("pe", lambda e, w=w, jj=jj, cc=cc, j=j: e.matmul(
                            PS[4 + cc][:, :T], w[:, jj * 512 + cc * 128:jj * 512 + (cc + 1) * 128], Hc(j)[:, :T],
                            start=(j == 0), stop=(j == NJ - 1)), [wr] + Hr(j), ["ps%d" % (4 + cc)])
            for cc in range(4):
                residual_prep(hf * 4 + cc, 4 + cc, T, 0.5 / ALPHA)
        ln_apply(l, lni, T, final)

    def in_proj(l, tile):
        T = tile.T
        g_ = ["BT"] if (l == 0 and tile.kind == "p" and tile.idx == 0) else []
        bank_i = [0]
        kcount = [0]
        for ocp in range(8):
            w, wr = load_w(("wina", l, ocp))
            for o2 in range(2):
                oc = ocp * 2 + o2
                bank = bank_i[0] % 4
                bank_i[0] += 1
                for kc in range(8):
                    S.add("pe", lambda e, w=w, o2=o2, kc=kc, bank=bank: e.matmul(
                        PS[bank][:, :T], w[:, (o2 * 8 + kc) * 128:(o2 * 8 + kc + 1) * 128], xbf[:, kc, :T],
                        start=(kc == 0), stop=(kc == 7)), [wr, "xbf%d" % kc], ["ps%d" % bank])
                    warm_pad()
                kind, i = OC_KIND[oc]
                psb = PS[bank]
                pr = "ps%d" % bank
                if kind == "u":
                    for s, (c0, n) in enumerate(tile.segs):
                        ub = s * (16 + n)
                        S.add("act", lambda e, i=i, ub=ub, c0=c0, n=n, psb=psb: e.copy(ubuf[:, i, ub + 16:ub + 16 + n], psb[:, c0:c0 + n]),
                              [pr], ["ubuf%d" % i])
                elif kind == "C":
                    S.add("act", lambda e, i=i, psb=psb: e.copy(Cbuf[:, i, :T], psb[:, :T]), [pr] + g_, ["Cbuf%d" % i])
                elif kind == "B":
                    S.add("act", lambda e, i=i, psb=psb: e.copy(Bbuf[:, i, :T], psb[:, :T]), [pr] + g_, ["Bbuf%d" % i])
                elif kind == "h":
                    for s, (c0, n) in enumerate(tile.segs):
                        zb_ = s * (2 + n)
                        S.add("dve", lambda e, i=i, zb_=zb_, c0=c0, n=n, psb=psb: e.tensor_tensor(
                            zbuf[:, i, zb_ + 2:zb_ + 2 + n], psb[:, c0:c0 + n], Cbuf[:, i, c0:c0 + n], ALU.mult),
                            [pr, "Cbuf%d" % i], ["zbuf%d" % i])
                elif kind == "q":
                    S.add("act", lambda e, i=i, psb=psb: e.activation(QT[:, i, :T], psb[:, :T], AF.Identity, scale=0.125), [pr] + g_, ["QT%d" % i])
                elif kind == "k":
                    ks = kcount[0] % 2
                    kcount[0] += 1
                    S.add("act", lambda e, ks=ks, psb=psb: e.copy(kst[ks][:, :T], psb[:, :T]), [pr] + g_, ["kst%d" % ks])
                    if tile.kind == "p":
                        S.add("dve", lambda e, i=i, ks=ks: e.tensor_copy(KT[:, i, tile.tok0:tile.tok0 + T], kst[ks][:, :T]),
                              ["kst%d" % ks], ["KT%d.%d" % (i, tile.idx)])
                        S.dma(okT[l, i * 128:(i + 1) * 128, tile.tok0:tile.tok0 + T], kst[ks][:, :T], ["kst%d" % ks], [], "kst%d" % ks)
                    else:
                        S.add("dve", lambda e, i=i, ks=ks: e.tensor_copy(KTs[:, i, :T], kst[ks][:, :T]), ["kst%d" % ks], ["KTs%d" % i])
                        S.dma(okTs[l, i * 128:(i + 1) * 128, :], kst[ks][:, :T], ["kst%d" % ks], [], "kst%d" % ks)
        wv = []
        for half in range(2):
            w, wr = load_w(("winv", l, half))
            wv.append((w, wr))
        if tile.kind == "p":
            subs = [(ts * 128, 128) for ts in range(T // 128)]
        else:
            subs = [(b * SS, SS) for b in range(SBATCH)]
        for si, (t0, n) in enumerate(subs):
            bank = 4 + (si % 4)
            for kc in range(8):
                w, wr = wv[kc // 4]
                S.add("pe", lambda e, w=w, kc=kc, bank=bank, t0=t0, n=n: e.matmul(
                    PS[bank][0:n, :], xbf[:, kc, t0:t0 + n], w[:, (kc % 4) * 512:(kc % 4 + 1) * 512],
                    start=(kc == 0), stop=(kc == 7)), [wr, "xbf%d" % kc], ["ps%d" % bank])
            vs_ = si % 2
            S.add("act", lambda e, vs_=vs_, bank=bank, n=n: e.copy(vst[vs_][0:n, :], PS[bank][0:n, :]), ["ps%d" % bank] + g_, ["vst%d" % vs_])
            if tile.kind == "p":
                kt = (tile.tok0 + t0) // 128
                S.add("dve", lambda e, kt=kt, vs_=vs_: e.tensor_copy(VV[:, kt, :], vst[vs_][:, :]), ["vst%d" % vs_], ["VV%d" % kt])
                S.dma(ov[l, tile.tok0 + t0:tile.tok0 + t0 + n, :], vst[vs_][0:n, :], ["vst%d" % vs_], [], "vst%d" % vs_)
            else:
                S.add("dve", lambda e, si=si, vs_=vs_, n=n: e.tensor_copy(Vs[0:n, si, :], vst[vs_][0:n, :]), ["vst%d" % vs_], ["Vs%d" % si])
                S.dma(ovs[l, t0:t0 + n, :], vst[vs_][0:n, :], ["vst%d" % vs_], [], "vst%d" % vs_)

    mix_post = []

    def mixers(l, tile, last_prompt):
        T = tile.T
        for i in range(2):
            ur = "ubuf%d" % i
            if tile.kind == "p" and tile.idx == 0:
                S.add("dve", lambda e, i=i: e.memset(ubuf[:, i, 0:16], 0.0), [], [ur])
            if tile.kind == "s":
                for s, (c0, n) in enumerate(tile.segs):
                    ub = s * (16 + n)
                    S.dma(ubuf[:, i, ub + 1:ub + 16], stp[l, s, i], [], [ur], "hl%d" % (s * 2 + i),
                          allow_slow_non_contiguous=True)
            wa, wb = POOL_W[2 * i], POOL_W[2 * i + 1]
            for s, (c0, n) in enumerate(tile.segs):
                ub = s * (16 + n)
                L = 15 + n

                def E(a, b, ub=ub, i=i):
                    return ubuf[:, i, ub + 1 + a:ub + 1 + b]

                S.add("dve", lambda e, E=E, L=L: e.tensor_tensor(pa[:, 1:L], E(1, L), E(0, L - 1), ALU.add), [ur], ["pa"])
                S.add("dve", lambda e, L=L: e.tensor_tensor(pb[:, 3:L], pa[:, 3:L], pa[:, 1:L - 2], ALU.add), ["pa"], ["pb"])
                if i == 1:
                    S.add("dve", lambda e, L=L: e.tensor_tensor(pa[:, 7:L], pb[:, 7:L], pb[:, 3:L - 4], ALU.add), ["pb"], ["pa"])
                    S.add("dve", lambda e, L=L: e.tensor_tensor(pb[:, 15:L], pa[:, 15:L], pa[:, 7:L - 8], ALU.add), ["pa"], ["pb"])
                S.add("dve", lambda e, E=E, L=L, i=i, c0=c0, n=n, wa=wa: e.scalar_tensor_tensor(
                    dpool[0:64, i, c0:c0 + n], pa[0:64, 15:L], 1.0 / wa, E(15, L)[0:64], ALU.mult, ALU.subtract),
                    ["pa", ur], ["dpool%d" % i])
                S.add("dve", lambda e, E=E, L=L, i=i, c0=c0, n=n, wb=wb: e.scalar_tensor_tensor(
                    dpool[64:128, i, c0:c0 + n], pb[64:128, 15:L], 1.0 / wb, E(15, L)[64:128], ALU.mult, ALU.subtract),
                    ["pb", ur], ["dpool%d" % i])
                if tile.kind == "p" and tile.idx == 0:
                    for (lo, hi, src) in ((0, 64, pa), (64, 128, pb)):
                        S.add("dve", lambda e, lo=lo, hi=hi, src=src, i=i: e.tensor_tensor(
                            tmpf[lo:hi, 0:15], src[lo:hi, 15:30], rcfix[lo:hi, i * 16:i * 16 + 15], ALU.mult),
                            ["pa", "pb", "rcfix"], ["tmpf"])
                        S.add("dve", lambda e, lo=lo, hi=hi, E=E, i=i: e.tensor_tensor(
                            dpool[lo:hi, i, 0:15], tmpf[lo:hi, 0:15], E(15, 30)[lo:hi], ALU.subtract),
                            ["tmpf", ur], ["dpool%d" % i])
                if tile.kind == "s":
                    S.dma(opools[l, s, i], E(n, n + 15), [ur], [], "so%d" % (s * 2 + i),
                          allow_slow_non_contiguous=True)
                elif last_prompt:
                    S.dma(opool[l, i], E(n, n + 15), [ur], [], "so%d" % (s * 2 + i),
                          allow_slow_non_contiguous=True)
                else:
                    S.add("dve", lambda e, E=E, n=n: e.tensor_copy(tmpf[:, 0:15], E(n, n + 15)), [ur], ["tmpf"])
                    S.add("dve", lambda e, E=E: e.tensor_copy(E(0, 15), tmpf[:, 0:15]), ["tmpf"], [ur])

            def post(i=i, bank=i):
                S.add("pe", lambda e: e.matmul(PS[bank][:, :T], PW[:, (l * 2 + i) * 128:(l * 2 + i + 1) * 128], dpool[:, i, :T],
                                               start=True, stop=True), ["PW", "dpool%d" % i], ["ps%d" % bank])
                S.add("act", lambda e: e.activation(mixT[:, i, :T], PS[bank][:, :T], AF.Identity, scale=pscale[:, l * 2 + i:l * 2 + i + 1]),
                      ["ps%d" % bank, "pscale"], ["mix%d" % i])

            mix_post.append(post)
        for i in range(2):
            zr = "zbuf%d" % i
            if tile.kind == "p" and tile.idx == 0:
                S.add("dve", lambda e, i=i: e.memset(zbuf[:, i, 0:2], 0.0), [], [zr])
            if tile.kind == "s":
                for s, (c0, n) in enumerate(tile.segs):
                    zb_ = s * (2 + n)
                    S.dma(zbuf[:, i, zb_:zb_ + 2], stc[l, s, i], [], [zr], "hl%d" % (4 + s * 2 + i),
                          allow_slow_non_contiguous=True)

            def wi(j, i=i):
                k = (l * 3 + j) * 2 + i
                return convw[:, k:k + 1]

            for s, (c0, n) in enumerate(tile.segs):
                zb_ = s * (2 + n)

                def Z(a, b, zb_=zb_, i=i):
                    return zbuf[:, i, zb_ + a:zb_ + b]

                acc = cacc[:, c0:c0 + n]
                S.add("dve", lambda e, Z=Z, n=n, acc=acc, wi=wi: e.tensor_scalar(acc, Z(2, 2 + n), wi(2), None, ALU.mult), [zr, "convw"], ["cacc"])
                S.add("dve", lambda e, Z=Z, n=n, acc=acc, wi=wi: e.scalar_tensor_tensor(acc, Z(1, 1 + n), wi(1), acc, ALU.mult, ALU.add),
                      [zr, "convw", "cacc"], ["cacc"])
                S.add("dve", lambda e, Z=Z, n=n, acc=acc, wi=wi: e.scalar_tensor_tensor(acc, Z(0, n), wi(0), acc, ALU.mult, ALU.add),
                      [zr, "convw", "cacc"], ["cacc"])
                if tile.kind == "s":
                    S.dma(oconvs[l, s, i], Z(n, n + 2), [zr], [], "so%d" % (4 + s * 2 + i), allow_slow_non_contiguous=True)
                elif last_prompt:
                    S.dma(oconv[l, i], Z(n, n + 2), [zr], [], "so%d" % (4 + s * 2 + i), allow_slow_non_contiguous=True)
                else:
                    S.add("dve", lambda e, Z=Z, n=n: e.tensor_copy(tmpf[:, 0:2], Z(n, n + 2)), [zr], ["tmpf"])
                    S.add("dve", lambda e, Z=Z: e.tensor_copy(Z(0, 2), tmpf[:, 0:2]), ["tmpf"], [zr])
            S.add("dve", lambda e, i=i: e.tensor_tensor(mixT[:, 2 + i, :T], cacc[:, :T], Bbuf[:, i, :T], ALU.mult),
                  ["cacc", "Bbuf%d" % i], ["mix%d" % (2 + i)])

    def attention(l, q0, Tq, ktiles_for_head, mix_c0):
        lam_init = 0.8 - 0.6 * math.exp(-0.3 * l)
        seq = []
        for h in range(NH):
            kts = ktiles_for_head(h)
            for idx, kt in enumerate(kts):
                seq.append((h, idx, len(kts), kt))
        cnt = [0]

        def emit_qk(item, k):
            h, idx, nk_t, kt = item
            if "load" in kt:
                kt["load"]()
            sset = k % 2
            nk = kt["nk"]
            for c in range(2):
                bank = 2 * sset + c
                bias = kt["bias"]
                S.add("pe", lambda e, kt=kt, c=c, bank=bank, nk=nk, h=h, bias=bias: e.matmul(
                    PS[bank][0:nk, :Tq], kt["kt"](c), QT[64 * c:64 * c + 64, h, q0:q0 + Tq], start=True, stop=(bias is None)),
                    kt["res"] + ["QT%d" % h], ["ps%d" % bank])
                if bias is not None:
                    S.add("pe", lambda e, bank=bank, nk=nk, bias=bias: e.matmul(
                        PS[bank][0:nk, :Tq], ident[0:nk, 0:nk], bias, start=False, stop=True),
                        ["ident", "BT"], ["ps%d" % bank])
            pt = k % 3
            S.add("act", lambda e, pt=pt, sset=sset, nk=nk: e.activation(PT[pt][0:nk, :, :Tq], PSA[0:nk, 2 * sset:2 * sset + 2, :Tq], AF.Exp),
                  ["ps%d" % (2 * sset), "ps%d" % (2 * sset + 1)], list(PTr[pt]))

        def emit_pv(item, k):
            h, idx, nk_t, kt = item
            nk = kt["nk"]
            pt = k % 3
            for c in range(2):
                S.add("pe", lambda e, kt=kt, c=c, nk=nk, pt=pt, idx=idx, nk_t=nk_t: e.matmul(
                    PS[4 + c][:, :Tq], kt["v"], PT[pt][0:nk, c, :Tq], start=(idx == 0), stop=(idx == nk_t - 1)),
                    kt["res"] + [PTr[pt][c]], ["ps%d" % (4 + c)])
                if c == 0:
                    S.add("pe", lambda e, c=c, nk=nk, pt=pt, idx=idx, nk_t=nk_t: e.matmul(
                        PS[6 + c][:, :Tq], ones1[0:nk, :], PT[pt][0:nk, c, :Tq], start=(idx == 0), stop=(idx == nk_t - 1)),
                        ["ones1", PTr[pt][c]], ["ps%d" % (6 + c)])
            acc = R1[0:nk, :Tq]
            if idx == 0:
                S.add("dve", lambda e, nk=nk, pt=pt, acc=acc: e.tensor_copy(acc, PT[pt][0:nk, 1, :Tq]), [PTr[pt][1]], R1r)
            else:
                S.add("dve", lambda e, nk=nk, pt=pt, acc=acc: e.tensor_tensor(acc, acc, PT[pt][0:nk, 1, :Tq], ALU.add),
                      [PTr[pt][1]] + R1r, R1r)
            if idx == nk_t - 1:
                finalize(h)

        def finalize(h):
            r0, r1, o0, o1, rs, sq = R0[:, :Tq], R1[:, :Tq], O0[:, :Tq], O1[:, :Tq], RS[:, :Tq], SQ[:, :Tq]
            S.add("dve", lambda e: e.tensor_copy(o0, PS[4][:, :Tq]), ["ps4"], O0r)
            S.add("dve", lambda e: e.tensor_copy(o1, PS[5][:, :Tq]), ["ps5"], O1r)
            S.add("dve", lambda e: e.tensor_copy(sq, r1), R1r, SQr)
            S.add("pe", lambda e: e.matmul(PS[7][:, :Tq], ones1, sq, start=True, stop=True), SQr + ["ones1"], ["ps7"])
            S.add("act", lambda e: e.activation(r0, PS[6][:, :Tq], AF.Ln), ["ps6"], R0r)
            S.add("act", lambda e: e.activation(r1, PS[7][:, :Tq], AF.Ln), ["ps7"], R1r)
            S.add("act", lambda e: e.activation(r0, r0, AF.Exp, scale=-1.0), R0r, R0r)
            S.add("act", lambda e: e.activation(r1, r1, AF.Exp, scale=-1.0), R1r, R1r)
            S.add("dve", lambda e: e.tensor_tensor(o0, o0, r0, ALU.mult), O0r + R0r, O0r)
            S.add("dve", lambda e: e.tensor_tensor(o1, o1, r1, ALU.mult), O1r + R1r, O1r)
            S.add("dve", lambda e: e.scalar_tensor_tensor(o0, o1, lamneg[:, l:l + 1], o0, ALU.mult, ALU.add),
                  O0r + O1r + ["lamneg"], O0r)
            S.add("act", lambda e: e.activation(sq, o0, AF.Square), O0r, SQr)
            k = cnt[0]
            cnt[0] += 1
            bank = 2 * (k % 2)
            S.add("pe", lambda e: e.matmul(PS[bank][:, :Tq], ones_rms, sq, start=True, stop=True), SQr + ["ones_rms"], ["ps%d" % bank])
            S.add("act", lambda e: e.activation(rs, PS[bank][:, :Tq], AF.Ln, bias=smalls[:, 13:14]), ["ps%d" % bank, "epsb"], RSr)
            S.add("act", lambda e: e.activation(rs, rs, AF.Exp, scale=-0.5), RSr, RSr)
            S.add("dve", lambda e: e.tensor_tensor(o0, o0, rs, ALU.mult), O0r + RSr, O0r)
            S.add("dve", lambda e: e.tensor_scalar(mixT[:, 4 + h, mix_c0:mix_c0 + Tq], o0, subln[:, l:l + 1], 1.0 - lam_init, ALU.mult, ALU.mult),
                  O0r + ["subln"], ["mix%d" % (4 + h)])

        prev = None
        for item in seq:
            k = cnt[0]
            cnt[0] += 1
            emit_qk(item, k)
            if prev is not None:
                emit_pv(*prev)
            prev = (item, k)
        emit_pv(*prev)

    def prompt_attention(l, tile):
        I = tile.idx

        def kts(h):
            out = []
            for j in range(4 * I + 4):
                m = j - 4 * I
                bias = BT[:, h * 5 + (m + 1), :tile.T] if m >= -1 else None
                out.append(dict(
                    kt=(lambda c, j=j, h=h: KT[64 * c:64 * c + 64, h, j * 128:(j + 1) * 128]),
                    v=VV[:, j, h * 128:(h + 1) * 128], nk=128, bias=bias,
                    res=["KT%d.%d" % (h, j // 4), "VV%d" % j]))
            return out

        attention(l, 0, tile.T, kts, 0)

    def sample_attention(l, tile):
        crc = [0]
        for b in range(SBATCH):
            def kts(h, b=b):
                out = []
                for g in range(NG):
                    slot = crc[0] % 2
                    crc[0] += 1
                    kbuf, vbuf, rr = CR[slot]

                    def load(g=g, h=h, kbuf=kbuf, vbuf=vbuf, rr=rr, slot=slot):
                        S.dma(kbuf, ckT_s[l, b, h][:, g * 512:(g + 1) * 512], cast_pieces["ckT%d%d" % (l, b)], [rr[0]], "crk%d" % slot)
                        S.dma(vbuf, cvt_s[l, b, h, g], cast_pieces["cvt%d%d" % (l, b)], [rr[1]], "crv%d" % slot)

                    for t in range(4):
                        last = (g == NG - 1 and t == 3)
                        d = dict(
                            kt=(lambda c, kbuf=kbuf, t=t: kbuf[64 * c:64 * c + 64, t * 128:(t + 1) * 128]),
                            v=vbuf[:, t * 128:(t + 1) * 128], nk=128,
                            bias=(BT[:, h * 5 + 0, 0:SS] if last else None), res=list(rr))
                        if t == 0:
                            d["load"] = load
                        out.append(d)
                out.append(dict(
                    kt=(lambda c, h=h: KTs[64 * c:64 * c + 64, h, b * SS:(b + 1) * SS]),
                    v=Vs[0:SS, b, h * 128:(h + 1) * 128], nk=SS,
                    bias=BT[0:SS, h * 5 + 1, 0:SS], res=["KTs%d" % h, "Vs%d" % b]))
                return out

            attention(l, b * SS, SS, kts, b * SS)

    def out_proj(l, T):
        bank_i = 0
        for cp in range(4):
            w, wr = load_w(("wout", l, cp))
            for c2 in range(2):
                c = cp * 2 + c2
                bank = bank_i % 4
                bank_i += 1
                for kc in range(8):
                    S.add("pe", lambda e, w=w, c2=c2, kc=kc, bank=bank: e.matmul(
                        PS[bank][:, :T], w[:, (c2 * 8 + kc) * 128:(c2 * 8 + kc + 1) * 128], mixT[:, kc, :T],
                        start=(kc == 0), stop=(kc == 7)), [wr, "mix%d" % kc], ["ps%d" % bank])
                residual_prep(c, bank, T, 1.0 / ALPHA)
        ln_apply(l, 1, T)

    def load_x(l, tile):
        T = tile.T
        for c in range(8):
            if tile.kind == "p":
                src = (xT if l == 0 else xs_p)[c * 128:(c + 1) * 128, tile.tok0:tile.tok0 + T]
                rd = [] if l == 0 else ["xsp%d.%d" % (tile.idx, c)]
            else:
                src = (xTs if l == 0 else xs_s)[c * 128:(c + 1) * 128, :]
                rd = [] if l == 0 else ["xss.%d" % c]
            S.dma(xres[:, c, :T], src, rd, ["xres%d" % c], "xl%d" % c)
            if (l, tile.kind, tile.idx) not in prefetched:
                S.add("dve", lambda e, c=c: e.tensor_copy(xbf[:, c, :T], xres[:, c, :T]), ["xres%d" % c], ["xbf%d" % c])

    def store_x(l, tile):
        T = tile.T
        for c in range(8):
            if tile.kind == "p":
                dst = (xs_p if l == 0 else oyT)[c * 128:(c + 1) * 128, tile.tok0:tile.tok0 + T]
                wr = ["xsp%d.%d" % (tile.idx, c)] if l == 0 else []
            else:
                dst = (xs_s if l == 0 else oyTs)[c * 128:(c + 1) * 128, :]
                wr = ["xss.%d" % c] if l == 0 else []
            S.dma(dst, STG[:, c, :T], Hr(2 * c, 2), wr, "xst%d" % c)

    tiles = [Tile("p", i, 512, i * 512, [(0, 512)]) for i in range(NT)]
    tiles.append(Tile("s", NT, TS, 0, [(b * SS, SS) for b in range(SBATCH)]))

    S.dma(ident, ident_d, [], ["ident"], "c_small", eng="pool", group=True)
    S.dma(PW, pwbd_d, [], ["PW"], "c_small", eng="pool", group=True)
    cast_weights(0)
    defer_mode[0] = True
    cast_caches(0)
    cast_weights(1)
    cast_caches(1)
    first = [True]
    for l in range(DEPTH):
        for ti, tile in enumerate(tiles):
            if stop is not None and (l, ti) == stop[0:2]:
                phases = stop[2]
            else:
                phases = 99
                if stop is not None and (l, ti) > stop[0:2]:
                    continue
            load_x(l, tile)
            hook = None
            if first[0]:
                first[0] = False
                setup()
                S.add("dve", lambda e: e.memset(smalls[:, 13:14], RMS_EPS), [], ["epsb"])
                hook = setup_bt
                if phases < 1:
                    setup_bt()
            if phases >= 1:
                ffn_ln(l, 0, 0, tile.T, mid_hook=hook)
            if phases >= 2:
                in_proj(l, tile)
            if phases >= 3:
                mixers(l, tile, tile.kind == "p" and tile.idx == NT - 1)
            if phases >= 4:
                if tile.kind == "p":
                    prompt_attention(l, tile)
                else:
                    sample_attention(l, tile)
            for p_ in mix_post:
                p_()
            del mix_post[:]
            if phases >= 5:
                out_proj(l, tile.T)
            next_x[0] = None
            if stop is None:
                if ti + 1 < len(tiles):
                    next_x[0] = (l, tiles[ti + 1])
                elif l + 1 < DEPTH:
                    next_x[0] = (l + 1, tiles[0])
            if phases >= 6:
                ffn_ln(l, 1, 2, tile.T, final=True)
            store_x(l, tile)
    if plan_only:
        return build_program(SEQ, PAST, stop, wplan=my_plan)
    assert not deferred or stop is not None, "deferred casts left: %d" % len(deferred)
    S.emit()
    return nc, S


_PROG_CACHE = {}
_DEBUG_STOP = None


def _prep_shared(ln_g, ln_b, w_ffn_in, w_ffn_out, w_in, w_out, pool_w, pool_scale, conv_w,
                 diff_lambda, subln_g, rel_bias):
    f = np.float32
    W = np.asarray(w_ffn_in, f).reshape(2, 2, 8, 128, 2, NJ, 128)
    wfi = np.ascontiguousarray(W.transpose(0, 1, 5, 3, 4, 2, 6)).reshape(2, 2, NJ, 128, 2048)
    W = np.asarray(w_ffn_out, f).reshape(2, 2, NJ, 128, 2, 512)
    wfo = np.ascontiguousarray(W.transpose(0, 1, 4, 3, 2, 5)).reshape(2, 2, 2, 128, NJ * 512)
    Wi = np.asarray(w_in, f).reshape(2, 8, 128, 2560)
    wa = np.stack([Wi[:, :, :, c0:c0 + 128] for c0 in OC_COLS], axis=1)
    wa = wa.reshape(2, 8, 2, 8, 128, 128).transpose(0, 1, 4, 2, 3, 5)
    wina = np.ascontiguousarray(wa).reshape(2, 8, 128, 2048)
    winv = np.ascontiguousarray(Wi[:, :, :, 2048:2560].transpose(0, 2, 1, 3)).reshape(2, 128, 4096)
    Wo = np.asarray(w_out, f).reshape(2, 8, 128, 4, 2, 128)
    wout = np.ascontiguousarray(Wo.transpose(0, 3, 2, 4, 1, 5)).reshape(2, 4, 128, 2048)
    lng = np.ascontiguousarray(np.asarray(ln_g, f).reshape(2, 3, 8, 128).transpose(3, 0, 1, 2)).reshape(128, 48)
    lnb = np.ascontiguousarray(np.asarray(ln_b, f).reshape(2, 3, 8, 128).transpose(3, 0, 1, 2)).reshape(128, 48)
    pw = np.asarray(pool_w, f)
    pwbd = np.zeros((2, 2, 128, 128), f)
    for l in range(2):
        for i in range(2):
            pwbd[l, i, 0:64, 0:64] = pw[l, 2 * i]
            pwbd[l, i, 64:128, 64:128] = pw[l, 2 * i + 1]
    pwbd = np.ascontiguousarray(pwbd.transpose(2, 0, 1, 3)).reshape(128, 512)
    pscale = np.ascontiguousarray(np.asarray(pool_scale, f).reshape(2, 2, 128).transpose(2, 0, 1)).reshape(128, 4)
    convw = np.ascontiguousarray(np.asarray(conv_w, f).reshape(2, 3, 2, 128).transpose(3, 0, 1, 2)).reshape(128, 12)
    dlam = np.ascontiguousarray(np.asarray(diff_lambda, f)).reshape(1, 512)
    sublng = np.ascontiguousarray(np.asarray(subln_g, f).T)
    relb = np.ascontiguousarray(np.asarray(rel_bias, f))
    oh, rc, ident = _host_consts()
    return dict(wfi=wfi, wfo=wfo, wina=wina, winv=winv, wout=wout, lng=lng, lnb=lnb, pwbd=pwbd,
                pscale=pscale, convw=convw, dlam=dlam, sublng=sublng, relb=relb, ohc=oh, rcfix=rc, identc=ident)


def kernel(x_prompt, x_sample, cache_k, cache_v, state_pool, state_conv,
           ln_g, ln_b, w_ffn_in, w_ffn_out, w_in, w_out,
           pool_w, pool_scale, conv_w, diff_lambda, subln_g, rel_bias):
    f = np.float32
    x_prompt = np.asarray(x_prompt, f)
    x_sample = np.asarray(x_sample, f)
    cache_k = np.asarray(cache_k, f)
    cache_v = np.asarray(cache_v, f)
    state_pool = np.asarray(state_pool, f)
    state_conv = np.asarray(state_conv, f)
    BATCH, SEQ = x_prompt.shape[:2]
    PAST = cache_k.shape[2]
    NG = PAST // 512
    assert BATCH == NCORES and x_sample.shape[0] == NCORES * SBATCH and x_sample.shape[1] == SS
    key = (SEQ, PAST)
    if key not in _PROG_CACHE:
        _PROG_CACHE[key] = build_program(SEQ, PAST, _DEBUG_STOP)[0]
    nc = _PROG_CACHE[key]
    shared = _prep_shared(ln_g, ln_b, w_ffn_in, w_ffn_out, w_in, w_out, pool_w, pool_scale, conv_w,
                          diff_lambda, subln_g, rel_bias)
    in_maps = []
    for c in range(NCORES):
        b0 = c * SBATCH
        m = dict(shared)
        m["xT"] = np.ascontiguousarray(x_prompt[c].T)
        m["xTs"] = np.ascontiguousarray(x_sample[b0:b0 + SBATCH].reshape(TS, D).T)
        ck = cache_k[:, b0:b0 + SBATCH]
        m["ckT"] = np.ascontiguousarray(ck.transpose(0, 1, 3, 4, 2))
        cv = cache_v[:, b0:b0 + SBATCH].reshape(2, SBATCH, NG, 4, 128, NH, 128)
        m["cvt"] = np.ascontiguousarray(cv.transpose(0, 1, 5, 2, 4, 3, 6)).reshape(2, SBATCH, NH, NG, 128, 512)
        m["stp"] = np.ascontiguousarray(state_pool[:, b0:b0 + SBATCH].reshape(2, SBATCH, 15, 2, 128).transpose(0, 1, 3, 4, 2))
        m["stc"] = np.ascontiguousarray(state_conv[:, b0:b0 + SBATCH].reshape(2, SBATCH, 2, 2, 128).transpose(0, 1, 3, 4, 2))
        in_maps.append(m)
    res = run_bass_kernel_spmd(nc, in_maps, core_ids=list(range(NCORES)))
    R = res.results
    y_prompt = np.stack([R[c]["oyT"].T for c in range(NCORES)]).astype(f)
    y_sample = np.concatenate([R[c]["oyTs"].T.reshape(SBATCH, SS, D) for c in range(NCORES)]).astype(f)
    nk_p = np.stack([R[c]["okT"].transpose(0, 2, 1).reshape(2, SEQ, NH, 128) for c in range(NCORES)], axis=1).astype(f)
    nv_p = np.stack([R[c]["ov"].reshape(2, SEQ, NH, 128) for c in range(NCORES)], axis=1).astype(f)
    np_p = np.stack([R[c]["opool"].transpose(0, 3, 1, 2).reshape(2, 15, 256) for c in range(NCORES)], axis=1).astype(f)
    nc_p = np.stack([R[c]["oconv"].transpose(0, 3, 1, 2).reshape(2, 2, 256) for c in range(NCORES)], axis=1).astype(f)
    nk_s = np.concatenate([R[c]["okTs"].transpose(0, 2, 1).reshape(2, SBATCH, SS, NH, 128) for c in range(NCORES)], axis=1).astype(f)
    nv_s = np.concatenate([R[c]["ovs"].reshape(2, SBATCH, SS, NH, 128) for c in range(NCORES)], axis=1).astype(f)
    np_s = np.concatenate([R[c]["opools"].transpose(0, 1, 4, 2, 3).reshape(2, SBATCH, 15, 256) for c in range(NCORES)], axis=1).astype(f)
    nc_s = np.concatenate([R[c]["oconvs"].transpose(0, 1, 4, 2, 3).reshape(2, SBATCH, 2, 256) for c in range(NCORES)], axis=1).astype(f)
    return (np.ascontiguousarray(y_prompt), np.ascontiguousarray(y_sample), np.ascontiguousarray(nk_p),
            np.ascontiguousarray(nv_p), np.ascontiguousarray(np_p), np.ascontiguousarray(nc_p),
            np.ascontiguousarray(nk_s), np.ascontiguousarray(nv_s), np.ascontiguousarray(np_s),
            np.ascontiguousarray(nc_s))
```

```python
import math
import os
from contextlib import ExitStack

import numpy as np
import concourse.bass as bass
import concourse.mybir as mybir
from concourse.bass_utils import run_bass_kernel_spmd

F32 = mybir.dt.float32
BF16 = mybir.dt.bfloat16
AF = mybir.ActivationFunctionType
ALU = mybir.AluOpType

NCORES = 8
D = 1024
DFF = 2816
NJ = 22
NH = 4
DEPTH = 2
SS = 32
SBATCH = 2
TS = SS * SBATCH
ALPHA = (2 * DEPTH) ** 0.25
LN_EPS2 = 1e-5 / (ALPHA * ALPHA)
RMS_EPS = 1e-5
ND = 1152
DOFF = 639
NEGM = -30000.0
OC_COLS = [0, 128, 512, 768, 256, 640, 896, 384, 1024, 1152, 1280, 1408, 1536, 1664, 1792, 1920]
OC_KIND = [("u", 0), ("u", 1), ("C", 0), ("h", 0), ("B", 0), ("C", 1), ("h", 1), ("B", 1),
           ("q", 0), ("q", 1), ("q", 2), ("q", 3), ("k", 0), ("k", 1), ("k", 2), ("k", 3)]
POOL_W = (2, 4, 8, 16)
EPOCH = 20000


class Op:
    __slots__ = ("eng", "fn", "deps", "signal", "dma_key", "sig", "idx", "eidx", "group")

    def __init__(self, eng, fn, dma_key):
        self.eng = eng
        self.fn = fn
        self.deps = set()
        self.signal = False
        self.dma_key = dma_key
        self.sig = None
        self.idx = 0
        self.group = False


class Sched:
    ENGS = ("pe", "act", "dve", "pool", "sp")

    def __init__(self, nc):
        self.nc = nc
        self.ops = []
        self.last_writer = {}
        self.readers = {}
        self.eng_count = {}

    def add(self, eng, fn, reads=(), writes=(), dma_key=None, group=False):
        op = Op(eng, fn, dma_key)
        op.group = group
        op.idx = len(self.ops)
        deps = op.deps
        lw = self.last_writer
        rd = self.readers
        eidx = self.eng_count.get(eng, 0)
        op.eidx = eidx
        self.eng_count[eng] = eidx + 1
        is_dma = dma_key is not None

        def consider(p, raw):
            if p is op:
                return
            if p.dma_key is None and not is_dma and p.eng == eng:
                if eng == "pe" or not raw or eidx - p.eidx > 2:
                    return
            deps.add(p)

        for r in reads:
            w = lw.get(r)
            if w is not None:
                consider(w, True)
        for w_ in writes:
            w = lw.get(w_)
            if w is not None:
                consider(w, False)
            for x in rd.get(w_, ()):
                consider(x, False)
        for d in deps:
            d.signal = True
        for r in reads:
            rd.setdefault(r, []).append(op)
        for w_ in writes:
            lw[w_] = op
            rd[w_] = []
        self.ops.append(op)
        return op

    def dma(self, out, in_, reads, writes, key, eng="sp", group=False, **kw):
        return self.add(eng, lambda e: e.dma_start(out=out, in_=in_, **kw), reads, writes,
                        dma_key=key, group=group)

    def emit(self):
        nc = self.nc
        ops = self.ops
        cnt = {e: 0 for e in self.ENGS}
        dcnt = {}
        for op in ops:
            if op.dma_key is not None:
                dcnt[op.dma_key] = dcnt.get(op.dma_key, 0) + 1
                op.sig = ["d", op.dma_key, 16 * dcnt[op.dma_key]]
            elif op.signal:
                c = cnt[op.eng]
                op.sig = ["e", (op.eng, c // EPOCH), (c % EPOCH) + 1]
                cnt[op.eng] = c + 1
        for op in ops:
            if op.dma_key is not None and op.group:
                op.sig[2] = 16 * dcnt[op.dma_key]
                for d in op.deps:
                    assert d.dma_key != op.dma_key, "group DMA depends on its own group: %s" % op.dma_key
        sem_names = sorted({op.sig[1] for op in ops if op.sig is not None}, key=str)
        self.nsems = len(sem_names)
        with ExitStack() as st:
            sems = {}
            for i, k in enumerate(sem_names):
                sems[k] = st.enter_context(nc.semaphore("s%d" % i))
            block = st.enter_context(nc.Block())
            by_eng = {e: [op for op in ops if op.eng == e] for e in self.ENGS}
            final_dma = dict((k, 16 * v) for k, v in dcnt.items())

            def run_engine(e, lst, is_last_waiter):
                waited = {}
                eng_epoch = {}
                for op in lst:
                    best = {}
                    for d in op.deps:
                        kk = (d.sig[0], d.sig[1])
                        if kk not in best or d.sig[2] > best[kk].sig[2]:
                            best[kk] = d
                    for d in sorted(best.values(), key=lambda o: o.idx):
                        kind, key, val = d.sig
                        if kind == "e":
                            pe_, ep = key
                            if eng_epoch.get(pe_, -1) > ep:
                                continue
                            if waited.get(key, 0) >= val:
                                continue
                            waited[key] = val
                            eng_epoch[pe_] = max(eng_epoch.get(pe_, -1), ep)
                        else:
                            if waited.get(key, 0) >= val:
                                continue
                            waited[key] = val
                        e.wait_ge(sems[key], val)
                    ins = op.fn(e)
                    if op.sig is not None:
                        ins.then_inc(sems[op.sig[1]], 16 if op.sig[0] == "d" else 1)
                if is_last_waiter:
                    for k, v in final_dma.items():
                        if waited.get(k, 0) < v:
                            e.wait_ge(sems[k], v)

            block.sync(lambda e: run_engine(e, by_eng["sp"], True))
            if by_eng["pe"]:
                block.tensor(lambda e: run_engine(e, by_eng["pe"], False))
            if by_eng["act"]:
                block.scalar(lambda e: run_engine(e, by_eng["act"], False))
            if by_eng["dve"]:
                block.vector(lambda e: run_engine(e, by_eng["dve"], False))
            if by_eng["pool"]:
                block.gpsimd(lambda e: run_engine(e, by_eng["pool"], False))


def _t5_bucket_np(rel):
    rel = np.asarray(rel, np.int64)
    nb, max_exact = 16, 8
    ret = (rel > 0).astype(np.int64) * nb
    n = np.abs(rel)
    nf = np.maximum(n, 1).astype(np.float32)
    lg = np.log(nf / np.float32(max_exact)).astype(np.float32)
    v = (lg / np.float32(math.log(128 / 8))).astype(np.float32) * np.float32(nb - max_exact)
    large = np.minimum(max_exact + v.astype(np.int32), nb - 1)
    return ret + np.where(n < max_exact, n, large)


def _host_consts():
    d = np.arange(ND) - DOFF
    bk = _t5_bucket_np(d)
    oh = np.zeros((32, ND), np.float32)
    oh[bk, np.arange(ND)] = 1.0
    rc = np.zeros((128, 32), np.float32)
    for i in range(2):
        for p in range(128):
            w = POOL_W[2 * i + (1 if p >= 64 else 0)]
            for t in range(16):
                rc[p, i * 16 + t] = 1.0 / min(w, t + 1)
    ident = np.eye(128, dtype=np.float32)
    return oh, rc, ident


class Tile:
    def __init__(self, kind, idx, T, tok0, segs):
        self.kind = kind
        self.idx = idx
        self.T = T
        self.tok0 = tok0
        self.segs = segs


class _DummySched:
    def __init__(self):
        self.ops = []

    def add(self, *a, **k):
        return None

    def dma(self, *a, **k):
        return None


def build_program(SEQ, PAST, stop=None, wplan=None):
    plan_only = wplan is None
    if plan_only:
        my_plan = []
    NT = SEQ // 512
    NG = PAST // 512
    NKT = SEQ // 128
    nc = bass.Bass("TRN2", target_bir_lowering=False)
    S = _DummySched() if plan_only else Sched(nc)

    def din(name, shape, dt=F32):
        return nc.dram_tensor(name, list(shape), dt, kind="ExternalInput").ap()

    def dout(name, shape):
        return nc.dram_tensor(name, list(shape), F32, kind="ExternalOutput").ap()

    def dscr(name, shape, dt):
        return nc.dram_tensor(name, list(shape), dt, kind="Internal").ap()

    def sb(name, shape, dt):
        return nc.alloc_sbuf_tensor(name, list(shape), dt).ap()

    xT = din("xT", [D, SEQ])
    xTs = din("xTs", [D, TS])
    ckT = din("ckT", [2, SBATCH, NH, 128, PAST])
    cvt = din("cvt", [2, SBATCH, NH, NG, 128, 512])
    stp = din("stp", [2, SBATCH, 2, 128, 15])
    stc = din("stc", [2, SBATCH, 2, 128, 2])
    lng_d = din("lng", [128, 48])
    lnb_d = din("lnb", [128, 48])
    wfi = din("wfi", [2, 2, NJ, 128, 2048])
    wfo = din("wfo", [2, 2, 2, 128, NJ * 512])
    wina = din("wina", [2, 8, 128, 2048])
    winv = din("winv", [2, 128, 4096])
    wout = din("wout", [2, 4, 128, 2048])
    pwbd_d = din("pwbd", [128, 4 * 128])
    pscale_d = din("pscale", [128, 4])
    convw_d = din("convw", [128, 12])
    dlam_d = din("dlam", [1, 512])
    subln_d = din("sublng", [128, 2])
    relb_d = din("relb", [32, 4])
    oh_d = din("ohc", [32, ND])
    rc_d = din("rcfix", [128, 32])
    ident_d = din("identc", [128, 128])

    oyT = dout("oyT", [D, SEQ])
    oyTs = dout("oyTs", [D, TS])
    okT = dout("okT", [2, 512, SEQ])
    ov = dout("ov", [2, SEQ, 512])
    opool = dout("opool", [2, 2, 128, 15])
    oconv = dout("oconv", [2, 2, 128, 2])
    okTs = dout("okTs", [2, 512, TS])
    ovs = dout("ovs", [2, TS, 512])
    opools = dout("opools", [2, SBATCH, 2, 128, 15])
    oconvs = dout("oconvs", [2, SBATCH, 2, 128, 2])

    wfi_s = dscr("wfi_s", [2, 2, NJ, 128, 2048], BF16)
    wfo_s = dscr("wfo_s", [2, 2, 2, 128, NJ * 512], BF16)
    wina_s = dscr("wina_s", [2, 8, 128, 2048], BF16)
    winv_s = dscr("winv_s", [2, 128, 4096], BF16)
    wout_s = dscr("wout_s", [2, 4, 128, 2048], BF16)
    ckT_s = dscr("ckT_s", [2, SBATCH, NH, 128, PAST], BF16)
    cvt_s = dscr("cvt_s", [2, SBATCH, NH, NG, 128, 512], BF16)
    xs_p = dscr("xs_p", [D, SEQ], F32)
    xs_s = dscr("xs_s", [D, TS], F32)
    fvd = dscr("fvd", [4, ND], F32)

    KT = sb("KT", [128, NH, SEQ], BF16)
    VV = sb("VV", [128, NKT, 512], BF16)
    xres = sb("xres", [128, 8, 512], F32)
    xbf = sb("xbf", [128, 8, 512], BF16)
    zsq = sb("zsq", [128, 8, 512], BF16)
    Hb = sb("Hb", [128, NJ * 512], BF16)
    QT = sb("QT", [128, NH, 512], BF16)
    mixT = sb("mixT", [128, 8, 512], BF16)
    BT = sb("BT", [128, 20, 512], BF16)
    NWR = 4
    WR = [sb("WR%d" % i, [128, 2048], BF16) for i in range(NWR)]
    ubuf = sb("ubuf", [128, 2, 528], F32)
    Bbuf = sb("Bbuf", [128, 2, 512], F32)
    Cbuf = sb("Cbuf", [128, 2, 512], F32)
    zbuf = sb("zbuf", [128, 2, 516], F32)
    pa = sb("pa", [128, 528], F32)
    pb = sb("pb", [128, 528], F32)
    dpool = sb("dpool", [128, 2, 512], BF16)
    cacc = sb("cacc", [128, 512], F32)
    kst = [sb("kst%d" % i, [128, 512], F32) for i in range(2)]
    vst = [sb("vst%d" % i, [128, 512], F32) for i in range(2)]
    KTs = sb("KTs", [128, NH, TS], BF16)
    Vs = sb("Vs", [32, SBATCH, 512], BF16)
    ones_ln = sb("ones_ln", [128, 128], BF16)
    ones1 = sb("ones1", [128, 128], BF16)
    ones_rms = sb("ones_rms", [128, 128], BF16)
    ident = sb("ident", [128, 128], BF16)
    PW = sb("PW", [128, 4 * 128], BF16)
    lng = sb("lng_sb", [128, 48], F32)
    lnb = sb("lnb_sb", [128, 48], F32)
    pscale = sb("pscale_sb", [128, 4], F32)
    convw = sb("convw_sb", [128, 12], F32)
    subln = sb("subln_sb", [128, 2], F32)
    rcfix = sb("rcfix_sb", [128, 32], F32)
    lamneg = sb("lamneg", [128, 2], F32)
    smalls = sb("smalls", [128, 16], F32)
    dlb = sb("dlb", [128, 512], F32)
    tmpf = sb("tmpf", [128, 16], F32)

    PSA = nc.alloc_psum_tensor("psall", [128, 8, 512], F32).ap()
    PS = [PSA[:, i, :] for i in range(8)]

    def Hc(j):
        return Hb[:, j * 512:(j + 1) * 512]

    def Hf(j):
        return Hb[:, j * 512:(j + 2) * 512].bitcast(F32)

    def Hr(j, n=1):
        return ["H%d" % (j + i) for i in range(n)]

    PT = [Hb[:, (2 * i) * 512:(2 * i + 2) * 512].rearrange("p (c t) -> p c t", c=2) for i in range(3)]
    PTr = [Hr(2 * i, 2) for i in range(3)]
    R0, R0r = Hf(6), Hr(6, 2)
    R1, R1r = Hf(8), Hr(8, 2)
    O0, O0r = Hf(10), Hr(10, 2)
    O1, O1r = Hf(12), Hr(12, 2)
    RS, RSr = Hf(14), Hr(14, 2)
    SQ, SQr = Hc(16), Hr(16, 1)
    LT0, LT0r = Hf(16), Hr(16, 2)
    LT1, LT1r = Hf(18), Hr(18, 2)
    LT2, LT2r = Hf(20), Hr(20, 2)
    CR = [(Hc(18), Hc(19), Hr(18, 2)), (Hc(20), Hc(21), Hr(20, 2))]

    cast_pieces = {}

    def wfi_group(j):
        p = j // 2
        return 0 if p < 1 else (1 if p < 5 else 2)

    cast_n = [0]

    deferred = []
    defer_mode = [False]

    def cast(dst, src, res, key=None):
        lst = cast_pieces.setdefault(res, [])
        nm = "%s.%d" % (res, len(lst))
        lst.append(nm)

        def emit(extra_reads=()):
            k = cast_n[0] % 2
            cast_n[0] += 1
            S.dma(dst, src, list(extra_reads), [nm, "cwslot%d" % k], "cw%d" % k, eng="pool")

        if defer_mode[0]:
            deferred.append(emit)
        else:
            emit()

    def cast_weights(l):
        for f in range(2):
            if f == 1:
                for i in range(4):
                    cast(wina_s[l, 2 * i:2 * i + 2], wina[l, 2 * i:2 * i + 2], "wina%d" % l)
                cast(winv_s[l], winv[l], "winv%d" % l)
                for i in range(2):
                    cast(wout_s[l, 2 * i:2 * i + 2], wout[l, 2 * i:2 * i + 2], "wout%d" % l)
            for i in range(NJ // 2):
                g = wfi_group(2 * i)
                cast(wfi_s[l, f, 2 * i:2 * i + 2], wfi[l, f, 2 * i:2 * i + 2], "wfi%d%d.%d" % (l, f, g))
            for hf in range(2):
                for i in range(2):
                    cast(wfo_s[l, f, hf][:, i * 5632:(i + 1) * 5632], wfo[l, f, hf][:, i * 5632:(i + 1) * 5632], "wfo%d%d" % (l, f))

    def cast_caches(l):
        for b in range(SBATCH):
            for h in range(NH):
                cast(ckT_s[l, b, h], ckT[l, b, h], "ckT%d%d" % (l, b))
                g0 = 0
                while g0 < NG:
                    g1 = min(NG, g0 + 4)
                    cast(cvt_s[l, b, h, g0:g1], cvt[l, b, h, g0:g1], "cvt%d%d" % (l, b))
                    g0 = g1

    def setup():
        for dst, src, nm in ((lng, lng_d, "lng"), (lnb, lnb_d, "lnb"), (pscale, pscale_d, "pscale"),
                             (convw, convw_d, "convw"), (subln, subln_d, "subln"), (rcfix, rc_d, "rcfix")):
            S.dma(dst, src, [], [nm], "setup", group=True)
        _skip = os.environ.get("K_SKIP", "").split(",")
        if "lam" not in _skip:
            S.dma(dlb, bass.AP(dlam_d.tensor, 0, [[0, 128], [1, 512]]), [], ["dlb"], "setup", group=True)
        S.add("dve", lambda e: e.memset(ones_ln, 1.0 / D), [], ["ones_ln"])
        S.add("dve", lambda e: e.memset(ones1, 1.0), [], ["ones1"])
        S.add("dve", lambda e: e.memset(ones_rms, 1.0 / 128.0), [], ["ones_rms"])
        for l in range(2):
            for k in range(2):
                a = dlb[:, (l * 4 + 2 * k) * 64:(l * 4 + 2 * k + 1) * 64]
                b_ = dlb[:, (l * 4 + 2 * k + 1) * 64:(l * 4 + 2 * k + 2) * 64]
                col = smalls[:, l * 4 + k:l * 4 + k + 1]
                S.add("dve", lambda e, a=a, b_=b_: e.tensor_tensor(pa[:, 0:64], a, b_, ALU.mult),
                      ["dlb"], ["pa"])
                S.add("dve", lambda e, col=col: e.reduce_sum(col, pa[:, 0:64], mybir.AxisListType.X), ["pa"], ["smalls"])
            S.add("act", lambda e, l=l: e.activation(smalls[:, l * 4 + 2:l * 4 + 4], smalls[:, l * 4:l * 4 + 2], AF.Exp),
                  ["smalls"], ["smalls"])
            lam_init = 0.8 - 0.6 * math.exp(-0.3 * l)
            S.add("dve", lambda e, l=l: e.tensor_tensor(smalls[:, 8 + l:9 + l], smalls[:, l * 4 + 3:l * 4 + 4],
                                                        smalls[:, l * 4 + 2:l * 4 + 3], ALU.subtract),
                  ["smalls"], ["smalls"])
            S.add("dve", lambda e, l=l, li=lam_init: e.tensor_scalar(lamneg[:, l:l + 1], smalls[:, 8 + l:9 + l], -li, None, ALU.add),
                  ["smalls"], ["lamneg"])

    def setup_bt():
        pages = []

        def add_pages(buf2d_bf16, ncols_bf16):
            for k in range(ncols_bf16 // 1024):
                pages.append(buf2d_bf16[:, k * 1024:(k + 1) * 1024].bitcast(F32))

        add_pages(KT.rearrange("p h s -> p (h s)"), NH * SEQ)
        add_pages(VV.rearrange("p k c -> p (k c)"), NKT * 512)
        add_pages(mixT.rearrange("p c t -> p (c t)"), 8 * 512)
        add_pages(QT.rearrange("p h t -> p (h t)"), NH * 512)
        relb = pages[0][0:32, 0:4]
        ohs = [pages[1][0:32, :], pages[2][0:32, :], pages[3][0:32, 0:ND - 1024]]
        fvs = [pages[4][0:4, :], pages[5][0:4, :], pages[6][0:4, :]]
        stg = pages[7:27]
        NSTG = len(stg)
        S.dma(relb, relb_d, [], ["pg0"], "setupbt", group=True)
        for i in range(3):
            n = min(512, ND - i * 512)
            S.dma(ohs[i][:, 0:n], oh_d[:, i * 512:i * 512 + n], [], ["pg%d" % (1 + i)], "setupbt", group=True)
        for i in range(3):
            n = min(512, ND - i * 512)
            S.add("pe", lambda e, i=i, n=n: e.matmul(PS[i][0:4, 0:n], relb, ohs[i][:, 0:n], start=True, stop=True),
                  ["pg0", "pg%d" % (1 + i)], ["ps%d" % i])
        S.add("act", lambda e: e.copy(smalls[0:4, 12:13], PS[0][0:4, 0:1]), ["ps0"], ["smalls_c"])
        for i in range(3):
            n = min(512, ND - i * 512)
            S.add("dve", lambda e, i=i, n=n: e.tensor_scalar(fvs[i][:, 0:n], PS[i][0:4, 0:n], smalls[0:4, 12:13], None, ALU.subtract),
                  ["ps%d" % i, "smalls_c"], ["pg%d" % (4 + i)])
            S.dma(fvd[:, i * 512:i * 512 + n], fvs[i][:, 0:n], ["pg%d" % (4 + i)], ["fvd%d" % i], "fvd_w%d" % i)
        k = 0
        for h in range(NH):
            for mi in range(5):
                m = mi - 1
                base = DOFF + 128 * m - 511
                s_ = k % NSTG
                src = bass.AP(fvd.tensor, h * ND + base, [[1, 128], [1, 512]])
                S.dma(stg[s_], src, ["fvd0", "fvd1", "fvd2"], ["hkp%d" % s_], "hk%d" % s_)
                bt = BT[:, h * 5 + mi, :]
                S.add("dve", lambda e, bt=bt, s_=s_: e.tensor_copy(bt, stg[s_][:, ::-1]), ["hkp%d" % s_], ["BT"])
                if m >= 0:
                    if m > 0:
                        S.add("dve", lambda e, h=h, mi=mi, m=m: e.memset(BT[0:64, h * 5 + mi, 0:128 * m], NEGM), [], ["BT"])
                    S.add("dve", lambda e, h=h, mi=mi, m=m: e.memset(BT[64:128, h * 5 + mi, 0:128 * m + 64], NEGM), [], ["BT"])
                k += 1

    ring = [0]
    wl_emitted = [0]
    DEFER_START = 60
    DEFER_EVERY = 3 if SEQ >= 2048 else 1
    LOOK = NWR - 2

    def wsrc(tag):
        kind = tag[0]
        if kind == "wfi":
            _, l, f, j = tag
            return wfi_s[l, f, j], 2048, "wfi%d%d.%d" % (l, f, wfi_group(j))
        if kind == "wfo":
            _, l, f, hf, jg, nj = tag
            return wfo_s[l, f, hf][:, jg * 512:(jg + nj) * 512], nj * 512, "wfo%d%d" % (l, f)
        if kind == "wina":
            _, l, ocp = tag
            return wina_s[l, ocp], 2048, "wina%d" % l
        if kind == "winv":
            _, l, half = tag
            return winv_s[l][:, half * 2048:(half + 1) * 2048], 2048, "winv%d" % l
        if kind == "wout":
            _, l, cp = tag
            return wout_s[l, cp], 2048, "wout%d" % l
        raise ValueError(tag)

    def load_w(tag):
        k = ring[0]
        ring[0] += 1
        if plan_only:
            my_plan.append(tag)
            return WR[k % NWR], "WR%d" % (k % NWR)
        upto = min(len(wplan), k + LOOK + 1)
        while wl_emitted[0] < upto:
            i = wl_emitted[0]
            wl_emitted[0] += 1
            src, ncols, res = wsrc(wplan[i])
            sl = i % NWR
            S.dma(WR[sl][:, 0:ncols], src, cast_pieces[res], ["WR%d" % sl], "WR%d" % sl)
            if deferred and i >= DEFER_START and (i - DEFER_START) % DEFER_EVERY == 0:
                deferred.pop(0)(["WR%d" % sl])
        assert wplan[k] == tag, (wplan[k], tag)
        return WR[k % NWR], "WR%d" % (k % NWR)

    STG = Hb[:, 0:16 * 512].bitcast(F32).rearrange("p (c t) -> p c t", c=8)

    next_x = [None]
    prefetched = set()

    def prefetch_x():
        if next_x[0] is None:
            return
        l2, t2 = next_x[0]
        T2 = t2.T
        for c in range(8):
            if t2.kind == "p":
                src = (xT if l2 == 0 else xs_p)[c * 128:(c + 1) * 128, t2.tok0:t2.tok0 + T2]
                rd = [] if l2 == 0 else ["xsp%d.%d" % (t2.idx, c)]
            else:
                src = (xTs if l2 == 0 else xs_s)[c * 128:(c + 1) * 128, :]
                rd = [] if l2 == 0 else ["xss.%d" % c]
            S.dma(xbf[:, c, :T2], src, rd, ["xbf%d" % c], "xp%d" % c, eng="pool")
        prefetched.add((l2, t2.kind, t2.idx))

    def ln_apply(l, lni, T, final=False):
        for c in range(8):
            S.add("pe", lambda e, c=c: e.matmul(PS[0][:, :T], ones_ln, xbf[:, c, :T], start=(c == 0), stop=(c == 7)),
                  ["xbf%d" % c, "ones_ln"], ["ps0"])
        for c in range(8):
            S.add("pe", lambda e, c=c: e.matmul(PS[1][:, :T], ones_ln, zsq[:, c, :T], start=(c == 0), stop=(c == 7)),
                  ["zsq%d" % c, "ones_ln"], ["ps1"])
        if final:
            prefetch_x()
        mean, m2, rstd = LT0[:, :T], LT1[:, :T], LT2[:, :T]
        S.add("act", lambda e: e.copy(mean, PS[0][:, :T]), ["ps0"], LT0r)
        S.add("act", lambda e: e.activation(m2, PS[0][:, :T], AF.Square), ["ps0"], LT1r)
        S.add("dve", lambda e: e.scalar_tensor_tensor(m2, PS[1][:, :T], LN_EPS2, m2, ALU.add, ALU.subtract), ["ps1"] + LT1r, LT1r)
        S.add("act", lambda e: e.activation(rstd, m2, AF.Ln), LT1r, LT2r)
        S.add("act", lambda e: e.activation(rstd, rstd, AF.Exp, scale=-0.5), LT2r, LT2r)
        for c in range(8):
            gi = (l * 3 + lni) * 8 + c
            xc = xres[:, c, :T]
            xr = "xres%d" % c
            S.add("dve", lambda e, xc=xc: e.tensor_tensor(xc, xc, mean, ALU.subtract), [xr] + LT0r, [xr])
            S.add("dve", lambda e, xc=xc: e.tensor_tensor(xc, xc, rstd, ALU.mult), [xr] + LT2r, [xr])
            if final:
                S.add("act", lambda e, xc=xc, c=c, gi=gi: e.activation(STG[:, c, :T], xc, AF.Identity, scale=lng[:, gi:gi + 1], bias=lnb[:, gi:gi + 1]),
                      [xr, "lng", "lnb"], Hr(2 * c, 2))
                continue
            S.add("act", lambda e, xc=xc, c=c, gi=gi: e.activation(xbf[:, c, :T], xc, AF.Identity, scale=lng[:, gi:gi + 1], bias=lnb[:, gi:gi + 1]),
                  [xr, "lng", "lnb"], ["xbf%d" % c])
        for c in range(8):
            if final:
                break
            gi = (l * 3 + lni) * 8 + c
            xc = xres[:, c, :T]
            xr = "xres%d" % c
            if c < 4:
                S.add("act", lambda e, xc=xc, gi=gi: e.activation(xc, xc, AF.Identity, scale=lng[:, gi:gi + 1], bias=lnb[:, gi:gi + 1]),
                      [xr, "lng", "lnb"], [xr])
            else:
                S.add("dve", lambda e, xc=xc, gi=gi: e.tensor_scalar(xc, xc, lng[:, gi:gi + 1], lnb[:, gi:gi + 1], ALU.mult, ALU.add),
                      [xr, "lng", "lnb"], [xr])

    def residual_prep(c, bank, T, coef):
        xc = xres[:, c, :T]
        S.add("dve", lambda e: e.scalar_tensor_tensor(xc, PS[bank][:, :T], coef, xc, ALU.mult, ALU.add),
              ["ps%d" % bank, "xres%d" % c], ["xres%d" % c])
        S.add("dve", lambda e: e.tensor_copy(xbf[:, c, :T], xc), ["xres%d" % c], ["xbf%d" % c])
        S.add("act", lambda e: e.activation(zsq[:, c, :T], xc, AF.Square), ["xres%d" % c], ["zsq%d" % c])

    def ffn_ln(l, f, lni, T, final=False, mid_hook=None):
        wres = "wfi%d%d" % (l, f)
        for j in range(NJ):
            w, wr = load_w(("wfi", l, f, j))
            bg, bu = 2 * (j % 4), 2 * (j % 4) + 1
            for gu, bank in ((0, bg), (1, bu)):
                for kc in range(8):
                    S.add("pe", lambda e, w=w, gu=gu, kc=kc, bank=bank: e.matmul(
                        PS[bank][:, :T], w[:, (gu * 8 + kc) * 128:(gu * 8 + kc + 1) * 128], xbf[:, kc, :T],
                        start=(kc == 0), stop=(kc == 7)), [wr, "xbf%d" % kc], ["ps%d" % bank])
            hj = Hc(j)[:, :T]
            S.add("act", lambda e, hj=hj, bg=bg: e.activation(hj, PS[bg][:, :T], AF.Silu), ["ps%d" % bg], Hr(j))
            S.add("dve", lambda e, hj=hj, bu=bu: e.tensor_tensor(hj, hj, PS[bu][:, :T], ALU.mult), ["ps%d" % bu] + Hr(j), Hr(j))
        if mid_hook is not None:
            mid_hook()
        wres = "wfo%d%d" % (l, f)
        for hf in range(2):
            for jg in range(0, NJ, 4):
                nj = min(4, NJ - jg)
                w, wr = load_w(("wfo", l, f, hf, jg, nj))
                for jj in range(nj):
                    j = jg + jj
                    for cc in range(4):
                        S.add("pe", lambda e, w=w, jj=jj, cc=cc, j=j: e.matmul(
                            PS[4 + cc][:, :T], w[:, jj * 512 + cc * 128:jj * 512 + (cc + 1) * 128], Hc(j)[:, :T],
                            start=(j == 0), stop=(j == NJ - 1)), [wr] + Hr(j), ["ps%d" % (4 + cc)])
            for cc in range(4):
                residual_prep(hf * 4 + cc, 4 + cc, T, 0.5 / ALPHA)
        ln_apply(l, lni, T, final)

    def in_proj(l, tile):
        T = tile.T
        g_ = ["BT"] if (l == 0 and tile.kind == "p" and tile.idx == 0) else []
        bank_i = [0]
        kcount = [0]
        for ocp in range(8):
            w, wr = load_w(("wina", l, ocp))
            for o2 in range(2):
                oc = ocp * 2 + o2
                bank = bank_i[0] % 4
                bank_i[0] += 1
                for kc in range(8):
                    S.add("pe", lambda e, w=w, o2=o2, kc=kc, bank=bank: e.matmul(
                        PS[bank][:, :T], w[:, (o2 * 8 + kc) * 128:(o2 * 8 + kc + 1) * 128], xbf[:, kc, :T],
                        start=(kc == 0), stop=(kc == 7)), [wr, "xbf%d" % kc], ["ps%d" % bank])
                kind, i = OC_KIND[oc]
                psb = PS[bank]
                pr = "ps%d" % bank
                if kind == "u":
                    for s, (c0, n) in enumerate(tile.segs):
                        ub = s * (16 + n)
                        S.add("act", lambda e, i=i, ub=ub, c0=c0, n=n, psb=psb: e.copy(ubuf[:, i, ub + 16:ub + 16 + n], psb[:, c0:c0 + n]),
                              [pr], ["ubuf%d" % i])
                elif kind == "C":
                    S.add("act", lambda e, i=i, psb=psb: e.copy(Cbuf[:, i, :T], psb[:, :T]), [pr] + g_, ["Cbuf%d" % i])
                elif kind == "B":
                    S.add("act", lambda e, i=i, psb=psb: e.copy(Bbuf[:, i, :T], psb[:, :T]), [pr] + g_, ["Bbuf%d" % i])
                elif kind == "h":
                    for s, (c0, n) in enumerate(tile.segs):
                        zb_ = s * (2 + n)
                        S.add("dve", lambda e, i=i, zb_=zb_, c0=c0, n=n, psb=psb: e.tensor_tensor(
                            zbuf[:, i, zb_ + 2:zb_ + 2 + n], psb[:, c0:c0 + n], Cbuf[:, i, c0:c0 + n], ALU.mult),
                            [pr, "Cbuf%d" % i], ["zbuf%d" % i])
                elif kind == "q":
                    S.add("act", lambda e, i=i, psb=psb: e.activation(QT[:, i, :T], psb[:, :T], AF.Identity, scale=0.125), [pr] + g_, ["QT%d" % i])
                elif kind == "k":
                    ks = kcount[0] % 2
                    kcount[0] += 1
                    S.add("act", lambda e, ks=ks, psb=psb: e.copy(kst[ks][:, :T], psb[:, :T]), [pr] + g_, ["kst%d" % ks])
                    if tile.kind == "p":
                        S.add("dve", lambda e, i=i, ks=ks: e.tensor_copy(KT[:, i, tile.tok0:tile.tok0 + T], kst[ks][:, :T]),
                              ["kst%d" % ks], ["KT%d.%d" % (i, tile.idx)])
                        S.dma(okT[l, i * 128:(i + 1) * 128, tile.tok0:tile.tok0 + T], kst[ks][:, :T], ["kst%d" % ks], [], "kst%d" % ks)
                    else:
                        S.add("dve", lambda e, i=i, ks=ks: e.tensor_copy(KTs[:, i, :T], kst[ks][:, :T]), ["kst%d" % ks], ["KTs%d" % i])
                        S.dma(okTs[l, i * 128:(i + 1) * 128, :], kst[ks][:, :T], ["kst%d" % ks], [], "kst%d" % ks)
        wv = []
        for half in range(2):
            w, wr = load_w(("winv", l, half))
            wv.append((w, wr))
        if tile.kind == "p":
            subs = [(ts * 128, 128) for ts in range(T // 128)]
        else:
            subs = [(b * SS, SS) for b in range(SBATCH)]
        for si, (t0, n) in enumerate(subs):
            bank = 4 + (si % 4)
            for kc in range(8):
                w, wr = wv[kc // 4]
                S.add("pe", lambda e, w=w, kc=kc, bank=bank, t0=t0, n=n: e.matmul(
                    PS[bank][0:n, :], xbf[:, kc, t0:t0 + n], w[:, (kc % 4) * 512:(kc % 4 + 1) * 512],
                    start=(kc == 0), stop=(kc == 7)), [wr, "xbf%d" % kc], ["ps%d" % bank])
            vs_ = si % 2
            S.add("act", lambda e, vs_=vs_, bank=bank, n=n: e.copy(vst[vs_][0:n, :], PS[bank][0:n, :]), ["ps%d" % bank] + g_, ["vst%d" % vs_])
            if tile.kind == "p":
                kt = (tile.tok0 + t0) // 128
                S.add("dve", lambda e, kt=kt, vs_=vs_: e.tensor_copy(VV[:, kt, :], vst[vs_][:, :]), ["vst%d" % vs_], ["VV%d" % kt])
                S.dma(ov[l, tile.tok0 + t0:tile.tok0 + t0 + n, :], vst[vs_][0:n, :], ["vst%d" % vs_], [], "vst%d" % vs_)
            else:
                S.add("dve", lambda e, si=si, vs_=vs_, n=n: e.tensor_copy(Vs[0:n, si, :], vst[vs_][0:n, :]), ["vst%d" % vs_], ["Vs%d" % si])
                S.dma(ovs[l, t0:t0 + n, :], vst[vs_][0:n, :], ["vst%d" % vs_], [], "vst%d" % vs_)

    mix_post = []

    def mixers(l, tile, last_prompt):
        T = tile.T
        for i in range(2):
            ur = "ubuf%d" % i
            if tile.kind == "p" and tile.idx == 0:
                S.add("dve", lambda e, i=i: e.memset(ubuf[:, i, 0:16], 0.0), [], [ur])
            if tile.kind == "s":
                for s, (c0, n) in enumerate(tile.segs):
                    ub = s * (16 + n)
                    S.dma(ubuf[:, i, ub + 1:ub + 16], stp[l, s, i], [], [ur], "hl%d" % (s * 2 + i),
                          allow_slow_non_contiguous=True)
            wa, wb = POOL_W[2 * i], POOL_W[2 * i + 1]
            for s, (c0, n) in enumerate(tile.segs):
                ub = s * (16 + n)
                L = 15 + n

                def E(a, b, ub=ub, i=i):
                    return ubuf[:, i, ub + 1 + a:ub + 1 + b]

                S.add("dve", lambda e, E=E, L=L: e.tensor_tensor(pa[:, 1:L], E(1, L), E(0, L - 1), ALU.add), [ur], ["pa"])
                S.add("dve", lambda e, L=L: e.tensor_tensor(pb[:, 3:L], pa[:, 3:L], pa[:, 1:L - 2], ALU.add), ["pa"], ["pb"])
                if i == 1:
                    S.add("dve", lambda e, L=L: e.tensor_tensor(pa[:, 7:L], pb[:, 7:L], pb[:, 3:L - 4], ALU.add), ["pb"], ["pa"])
                    S.add("dve", lambda e, L=L: e.tensor_tensor(pb[:, 15:L], pa[:, 15:L], pa[:, 7:L - 8], ALU.add), ["pa"], ["pb"])
                S.add("dve", lambda e, E=E, L=L, i=i, c0=c0, n=n, wa=wa: e.scalar_tensor_tensor(
                    dpool[0:64, i, c0:c0 + n], pa[0:64, 15:L], 1.0 / wa, E(15, L)[0:64], ALU.mult, ALU.subtract),
                    ["pa", ur], ["dpool%d" % i])
                S.add("dve", lambda e, E=E, L=L, i=i, c0=c0, n=n, wb=wb: e.scalar_tensor_tensor(
                    dpool[64:128, i, c0:c0 + n], pb[64:128, 15:L], 1.0 / wb, E(15, L)[64:128], ALU.mult, ALU.subtract),
                    ["pb", ur], ["dpool%d" % i])
                if tile.kind == "p" and tile.idx == 0:
                    for (lo, hi, src) in ((0, 64, pa), (64, 128, pb)):
                        S.add("dve", lambda e, lo=lo, hi=hi, src=src, i=i: e.tensor_tensor(
                            tmpf[lo:hi, 0:15], src[lo:hi, 15:30], rcfix[lo:hi, i * 16:i * 16 + 15], ALU.mult),
                            ["pa", "pb", "rcfix"], ["tmpf"])
                        S.add("dve", lambda e, lo=lo, hi=hi, E=E, i=i: e.tensor_tensor(
                            dpool[lo:hi, i, 0:15], tmpf[lo:hi, 0:15], E(15, 30)[lo:hi], ALU.subtract),
                            ["tmpf", ur], ["dpool%d" % i])
                if tile.kind == "s":
                    S.dma(opools[l, s, i], E(n, n + 15), [ur], [], "so%d" % (s * 2 + i),
                          allow_slow_non_contiguous=True)
                elif last_prompt:
                    S.dma(opool[l, i], E(n, n + 15), [ur], [], "so%d" % (s * 2 + i),
                          allow_slow_non_contiguous=True)
                else:
                    S.add("dve", lambda e, E=E, n=n: e.tensor_copy(tmpf[:, 0:15], E(n, n + 15)), [ur], ["tmpf"])
                    S.add("dve", lambda e, E=E: e.tensor_copy(E(0, 15), tmpf[:, 0:15]), ["tmpf"], [ur])

            def post(i=i, bank=i):
                S.add("pe", lambda e: e.matmul(PS[bank][:, :T], PW[:, (l * 2 + i) * 128:(l * 2 + i + 1) * 128], dpool[:, i, :T],
                                               start=True, stop=True), ["PW", "dpool%d" % i], ["ps%d" % bank])
                S.add("act", lambda e: e.activation(mixT[:, i, :T], PS[bank][:, :T], AF.Identity, scale=pscale[:, l * 2 + i:l * 2 + i + 1]),
                      ["ps%d" % bank, "pscale"], ["mix%d" % i])

            mix_post.append(post)
        for i in range(2):
            zr = "zbuf%d" % i
            if tile.kind == "p" and tile.idx == 0:
                S.add("dve", lambda e, i=i: e.memset(zbuf[:, i, 0:2], 0.0), [], [zr])
            if tile.kind == "s":
                for s, (c0, n) in enumerate(tile.segs):
                    zb_ = s * (2 + n)
                    S.dma(zbuf[:, i, zb_:zb_ + 2], stc[l, s, i], [], [zr], "hl%d" % (4 + s * 2 + i),
                          allow_slow_non_contiguous=True)

            def wi(j, i=i):
                k = (l * 3 + j) * 2 + i
                return convw[:, k:k + 1]

            for s, (c0, n) in enumerate(tile.segs):
                zb_ = s * (2 + n)

                def Z(a, b, zb_=zb_, i=i):
                    return zbuf[:, i, zb_ + a:zb_ + b]

                acc = cacc[:, c0:c0 + n]
                S.add("dve", lambda e, Z=Z, n=n, acc=acc, wi=wi: e.tensor_scalar(acc, Z(2, 2 + n), wi(2), None, ALU.mult), [zr, "convw"], ["cacc"])
                S.add("dve", lambda e, Z=Z, n=n, acc=acc, wi=wi: e.scalar_tensor_tensor(acc, Z(1, 1 + n), wi(1), acc, ALU.mult, ALU.add),
                      [zr, "convw", "cacc"], ["cacc"])
                S.add("dve", lambda e, Z=Z, n=n, acc=acc, wi=wi: e.scalar_tensor_tensor(acc, Z(0, n), wi(0), acc, ALU.mult, ALU.add),
                      [zr, "convw", "cacc"], ["cacc"])
                if tile.kind == "s":
                    S.dma(oconvs[l, s, i], Z(n, n + 2), [zr], [], "so%d" % (4 + s * 2 + i), allow_slow_non_contiguous=True)
                elif last_prompt:
                    S.dma(oconv[l, i], Z(n, n + 2), [zr], [], "so%d" % (4 + s * 2 + i), allow_slow_non_contiguous=True)
                else:
                    S.add("dve", lambda e, Z=Z, n=n: e.tensor_copy(tmpf[:, 0:2], Z(n, n + 2)), [zr], ["tmpf"])
                    S.add("dve", lambda e, Z=Z: e.tensor_copy(Z(0, 2), tmpf[:, 0:2]), ["tmpf"], [zr])
            S.add("dve", lambda e, i=i: e.tensor_tensor(mixT[:, 2 + i, :T], cacc[:, :T], Bbuf[:, i, :T], ALU.mult),
                  ["cacc", "Bbuf%d" % i], ["mix%d" % (2 + i)])

    def attention(l, q0, Tq, ktiles_for_head, mix_c0):
        lam_init = 0.8 - 0.6 * math.exp(-0.3 * l)
        seq = []
        for h in range(NH):
            kts = ktiles_for_head(h)
            for idx, kt in enumerate(kts):
                seq.append((h, idx, len(kts), kt))
        cnt = [0]

        def emit_qk(item, k):
            h, idx, nk_t, kt = item
            if "load" in kt:
                kt["load"]()
            sset = k % 2
            nk = kt["nk"]
            for c in range(2):
                bank = 2 * sset + c
                bias = kt["bias"]
                S.add("pe", lambda e, kt=kt, c=c, bank=bank, nk=nk, h=h, bias=bias: e.matmul(
                    PS[bank][0:nk, :Tq], kt["kt"](c), QT[64 * c:64 * c + 64, h, q0:q0 + Tq], start=True, stop=(bias is None)),
                    kt["res"] + ["QT%d" % h], ["ps%d" % bank])
                if bias is not None:
                    S.add("pe", lambda e, bank=bank, nk=nk, bias=bias: e.matmul(
                        PS[bank][0:nk, :Tq], ident[0:nk, 0:nk], bias, start=False, stop=True),
                        ["ident", "BT"], ["ps%d" % bank])
            pt = k % 3
            S.add("act", lambda e, pt=pt, sset=sset, nk=nk: e.activation(PT[pt][0:nk, :, :Tq], PSA[0:nk, 2 * sset:2 * sset + 2, :Tq], AF.Exp),
                  ["ps%d" % (2 * sset), "ps%d" % (2 * sset + 1)], list(PTr[pt]))

        def emit_pv(item, k):
            h, idx, nk_t, kt = item
            nk = kt["nk"]
            pt = k % 3
            for c in range(2):
                S.add("pe", lambda e, kt=kt, c=c, nk=nk, pt=pt, idx=idx, nk_t=nk_t: e.matmul(
                    PS[4 + c][:, :Tq], kt["v"], PT[pt][0:nk, c, :Tq], start=(idx == 0), stop=(idx == nk_t - 1)),
                    kt["res"] + [PTr[pt][c]], ["ps%d" % (4 + c)])
                S.add("pe", lambda e, c=c, nk=nk, pt=pt, idx=idx, nk_t=nk_t: e.matmul(
                    PS[6 + c][:, :Tq], ones1[0:nk, :], PT[pt][0:nk, c, :Tq], start=(idx == 0), stop=(idx == nk_t - 1)),
                    ["ones1", PTr[pt][c]], ["ps%d" % (6 + c)])
            if idx == nk_t - 1:
                finalize(h)

        def finalize(h):
            r0, r1, o0, o1, rs, sq = R0[:, :Tq], R1[:, :Tq], O0[:, :Tq], O1[:, :Tq], RS[:, :Tq], SQ[:, :Tq]
            S.add("dve", lambda e: e.tensor_copy(o0, PS[4][:, :Tq]), ["ps4"], O0r)
            S.add("dve", lambda e: e.tensor_copy(o1, PS[5][:, :Tq]), ["ps5"], O1r)
            S.add("act", lambda e: e.activation(r0, PS[6][:, :Tq], AF.Ln), ["ps6"], R0r)
            S.add("act", lambda e: e.activation(r1, PS[7][:, :Tq], AF.Ln), ["ps7"], R1r)
            S.add("act", lambda e: e.activation(r0, r0, AF.Exp, scale=-1.0), R0r, R0r)
            S.add("act", lambda e: e.activation(r1, r1, AF.Exp, scale=-1.0), R1r, R1r)
            S.add("dve", lambda e: e.tensor_tensor(o0, o0, r0, ALU.mult), O0r + R0r, O0r)
            S.add("dve", lambda e: e.tensor_tensor(o1, o1, r1, ALU.mult), O1r + R1r, O1r)
            S.add("dve", lambda e: e.scalar_tensor_tensor(o0, o1, lamneg[:, l:l + 1], o0, ALU.mult, ALU.add),
                  O0r + O1r + ["lamneg"], O0r)
            S.add("act", lambda e: e.activation(sq, o0, AF.Square), O0r, SQr)
            k = cnt[0]
            cnt[0] += 1
            bank = 2 * (k % 2)
            S.add("pe", lambda e: e.matmul(PS[bank][:, :Tq], ones_rms, sq, start=True, stop=True), SQr + ["ones_rms"], ["ps%d" % bank])
            S.add("act", lambda e: e.activation(rs, PS[bank][:, :Tq], AF.Ln, bias=smalls[:, 13:14]), ["ps%d" % bank, "epsb"], RSr)
            S.add("act", lambda e: e.activation(rs, rs, AF.Exp, scale=-0.5), RSr, RSr)
            S.add("dve", lambda e: e.tensor_tensor(o0, o0, rs, ALU.mult), O0r + RSr, O0r)
            S.add("dve", lambda e: e.tensor_scalar(mixT[:, 4 + h, mix_c0:mix_c0 + Tq], o0, subln[:, l:l + 1], 1.0 - lam_init, ALU.mult, ALU.mult),
                  O0r + ["subln"], ["mix%d" % (4 + h)])

        prev = None
        for item in seq:
            k = cnt[0]
            cnt[0] += 1
            emit_qk(item, k)
            if prev is not None:
                emit_pv(*prev)
            prev = (item, k)
        emit_pv(*prev)

    def prompt_attention(l, tile):
        I = tile.idx

        def kts(h):
            out = []
            for j in range(4 * I + 4):
                m = j - 4 * I
                bias = BT[:, h * 5 + (m + 1), :tile.T] if m >= -1 else None
                out.append(dict(
                    kt=(lambda c, j=j, h=h: KT[64 * c:64 * c + 64, h, j * 128:(j + 1) * 128]),
                    v=VV[:, j, h * 128:(h + 1) * 128], nk=128, bias=bias,
                    res=["KT%d.%d" % (h, j // 4), "VV%d" % j]))
            return out

        attention(l, 0, tile.T, kts, 0)

    def sample_attention(l, tile):
        crc = [0]
        for b in range(SBATCH):
            def kts(h, b=b):
                out = []
                for g in range(NG):
                    slot = crc[0] % 2
                    crc[0] += 1
                    kbuf, vbuf, rr = CR[slot]

                    def load(g=g, h=h, kbuf=kbuf, vbuf=vbuf, rr=rr, slot=slot):
                        S.dma(kbuf, ckT_s[l, b, h][:, g * 512:(g + 1) * 512], cast_pieces["ckT%d%d" % (l, b)], [rr[0]], "crk%d" % slot)
                        S.dma(vbuf, cvt_s[l, b, h, g], cast_pieces["cvt%d%d" % (l, b)], [rr[1]], "crv%d" % slot)

                    for t in range(4):
                        last = (g == NG - 1 and t == 3)
                        d = dict(
                            kt=(lambda c, kbuf=kbuf, t=t: kbuf[64 * c:64 * c + 64, t * 128:(t + 1) * 128]),
                            v=vbuf[:, t * 128:(t + 1) * 128], nk=128,
                            bias=(BT[:, h * 5 + 0, 0:SS] if last else None), res=list(rr))
                        if t == 0:
                            d["load"] = load
                        out.append(d)
                out.append(dict(
                    kt=(lambda c, h=h: KTs[64 * c:64 * c + 64, h, b * SS:(b + 1) * SS]),
                    v=Vs[0:SS, b, h * 128:(h + 1) * 128], nk=SS,
                    bias=BT[0:SS, h * 5 + 1, 0:SS], res=["KTs%d" % h, "Vs%d" % b]))
                return out

            attention(l, b * SS, SS, kts, b * SS)

    def out_proj(l, T):
        bank_i = 0
        for cp in range(4):
            w, wr = load_w(("wout", l, cp))
            for c2 in range(2):
                c = cp * 2 + c2
                bank = bank_i % 4
                bank_i += 1
                for kc in range(8):
                    S.add("pe", lambda e, w=w, c2=c2, kc=kc, bank=bank: e.matmul(
                        PS[bank][:, :T], w[:, (c2 * 8 + kc) * 128:(c2 * 8 + kc + 1) * 128], mixT[:, kc, :T],
                        start=(kc == 0), stop=(kc == 7)), [wr, "mix%d" % kc], ["ps%d" % bank])
                residual_prep(c, bank, T, 1.0 / ALPHA)
        ln_apply(l, 1, T)

    def load_x(l, tile):
        T = tile.T
        for c in range(8):
            if tile.kind == "p":
                src = (xT if l == 0 else xs_p)[c * 128:(c + 1) * 128, tile.tok0:tile.tok0 + T]
                rd = [] if l == 0 else ["xsp%d.%d" % (tile.idx, c)]
            else:
                src = (xTs if l == 0 else xs_s)[c * 128:(c + 1) * 128, :]
                rd = [] if l == 0 else ["xss.%d" % c]
            S.dma(xres[:, c, :T], src, rd, ["xres%d" % c], "xl%d" % c)
            if (l, tile.kind, tile.idx) not in prefetched:
                S.add("dve", lambda e, c=c: e.tensor_copy(xbf[:, c, :T], xres[:, c, :T]), ["xres%d" % c], ["xbf%d" % c])

    def store_x(l, tile):
        T = tile.T
        for c in range(8):
            if tile.kind == "p":
                dst = (xs_p if l == 0 else oyT)[c * 128:(c + 1) * 128, tile.tok0:tile.tok0 + T]
                wr = ["xsp%d.%d" % (tile.idx, c)] if l == 0 else []
            else:
                dst = (xs_s if l == 0 else oyTs)[c * 128:(c + 1) * 128, :]
                wr = ["xss.%d" % c] if l == 0 else []
            S.dma(dst, STG[:, c, :T], Hr(2 * c, 2), wr, "xst%d" % c)

    tiles = [Tile("p", i, 512, i * 512, [(0, 512)]) for i in range(NT)]
    tiles.append(Tile("s", NT, TS, 0, [(b * SS, SS) for b in range(SBATCH)]))

    S.dma(ident, ident_d, [], ["ident"], "c_small", eng="pool", group=True)
    S.dma(PW, pwbd_d, [], ["PW"], "c_small", eng="pool", group=True)
    cast_weights(0)
    defer_mode[0] = True
    cast_caches(0)
    cast_weights(1)
    cast_caches(1)
    first = [True]
    for l in range(DEPTH):
        for ti, tile in enumerate(tiles):
            if stop is not None and (l, ti) == stop[0:2]:
                phases = stop[2]
            else:
                phases = 99
                if stop is not None and (l, ti) > stop[0:2]:
                    continue
            load_x(l, tile)
            hook = None
            if first[0]:
                first[0] = False
                setup()
                S.add("dve", lambda e: e.memset(smalls[:, 13:14], RMS_EPS), [], ["epsb"])
                hook = setup_bt
                if phases < 1:
                    setup_bt()
            if phases >= 1:
                ffn_ln(l, 0, 0, tile.T, mid_hook=hook)
            if phases >= 2:
                in_proj(l, tile)
            if phases >= 3:
                mixers(l, tile, tile.kind == "p" and tile.idx == NT - 1)
            if phases >= 4:
                if tile.kind == "p":
                    prompt_attention(l, tile)
                else:
                    sample_attention(l, tile)
            for p_ in mix_post:
                p_()
            del mix_post[:]
            if phases >= 5:
                out_proj(l, tile.T)
            next_x[0] = None
            if stop is None:
                if ti + 1 < len(tiles):
                    next_x[0] = (l, tiles[ti + 1])
                elif l + 1 < DEPTH:
                    next_x[0] = (l + 1, tiles[0])
            if phases >= 6:
                ffn_ln(l, 1, 2, tile.T, final=True)
            store_x(l, tile)
    if plan_only:
        return build_program(SEQ, PAST, stop, wplan=my_plan)
    assert not deferred or stop is not None, "deferred casts left: %d" % len(deferred)
    S.emit()
    return nc, S


_PROG_CACHE = {}
_DEBUG_STOP = None


def _prep_shared(ln_g, ln_b, w_ffn_in, w_ffn_out, w_in, w_out, pool_w, pool_scale, conv_w,
                 diff_lambda, subln_g, rel_bias):
    f = np.float32
    W = np.asarray(w_ffn_in, f).reshape(2, 2, 8, 128, 2, NJ, 128)
    wfi = np.ascontiguousarray(W.transpose(0, 1, 5, 3, 4, 2, 6)).reshape(2, 2, NJ, 128, 2048)
    W = np.asarray(w_ffn_out, f).reshape(2, 2, NJ, 128, 2, 512)
    wfo = np.ascontiguousarray(W.transpose(0, 1, 4, 3, 2, 5)).reshape(2, 2, 2, 128, NJ * 512)
    Wi = np.asarray(w_in, f).reshape(2, 8, 128, 2560)
    wa = np.stack([Wi[:, :, :, c0:c0 + 128] for c0 in OC_COLS], axis=1)
    wa = wa.reshape(2, 8, 2, 8, 128, 128).transpose(0, 1, 4, 2, 3, 5)
    wina = np.ascontiguousarray(wa).reshape(2, 8, 128, 2048)
    winv = np.ascontiguousarray(Wi[:, :, :, 2048:2560].transpose(0, 2, 1, 3)).reshape(2, 128, 4096)
    Wo = np.asarray(w_out, f).reshape(2, 8, 128, 4, 2, 128)
    wout = np.ascontiguousarray(Wo.transpose(0, 3, 2, 4, 1, 5)).reshape(2, 4, 128, 2048)
    lng = np.ascontiguousarray(np.asarray(ln_g, f).reshape(2, 3, 8, 128).transpose(3, 0, 1, 2)).reshape(128, 48)
    lnb = np.ascontiguousarray(np.asarray(ln_b, f).reshape(2, 3, 8, 128).transpose(3, 0, 1, 2)).reshape(128, 48)
    pw = np.asarray(pool_w, f)
    pwbd = np.zeros((2, 2, 128, 128), f)
    for l in range(2):
        for i in range(2):
            pwbd[l, i, 0:64, 0:64] = pw[l, 2 * i]
            pwbd[l, i, 64:128, 64:128] = pw[l, 2 * i + 1]
    pwbd = np.ascontiguousarray(pwbd.transpose(2, 0, 1, 3)).reshape(128, 512)
    pscale = np.ascontiguousarray(np.asarray(pool_scale, f).reshape(2, 2, 128).transpose(2, 0, 1)).reshape(128, 4)
    convw = np.ascontiguousarray(np.asarray(conv_w, f).reshape(2, 3, 2, 128).transpose(3, 0, 1, 2)).reshape(128, 12)
    dlam = np.ascontiguousarray(np.asarray(diff_lambda, f)).reshape(1, 512)
    sublng = np.ascontiguousarray(np.asarray(subln_g, f).T)
    relb = np.ascontiguousarray(np.asarray(rel_bias, f))
    oh, rc, ident = _host_consts()
    return dict(wfi=wfi, wfo=wfo, wina=wina, winv=winv, wout=wout, lng=lng, lnb=lnb, pwbd=pwbd,
                pscale=pscale, convw=convw, dlam=dlam, sublng=sublng, relb=relb, ohc=oh, rcfix=rc, identc=ident)


def kernel(x_prompt, x_sample, cache_k, cache_v, state_pool, state_conv,
           ln_g, ln_b, w_ffn_in, w_ffn_out, w_in, w_out,
           pool_w, pool_scale, conv_w, diff_lambda, subln_g, rel_bias):
    f = np.float32
    x_prompt = np.asarray(x_prompt, f)
    x_sample = np.asarray(x_sample, f)
    cache_k = np.asarray(cache_k, f)
    cache_v = np.asarray(cache_v, f)
    state_pool = np.asarray(state_pool, f)
    state_conv = np.asarray(state_conv, f)
    BATCH, SEQ = x_prompt.shape[:2]
    PAST = cache_k.shape[2]
    NG = PAST // 512
    assert BATCH == NCORES and x_sample.shape[0] == NCORES * SBATCH and x_sample.shape[1] == SS
    key = (SEQ, PAST)
    if key not in _PROG_CACHE:
        _PROG_CACHE[key] = build_program(SEQ, PAST, _DEBUG_STOP)[0]
    nc = _PROG_CACHE[key]
    shared = _prep_shared(ln_g, ln_b, w_ffn_in, w_ffn_out, w_in, w_out, pool_w, pool_scale, conv_w,
                          diff_lambda, subln_g, rel_bias)
    in_maps = []
    for c in range(NCORES):
        b0 = c * SBATCH
        m = dict(shared)
        m["xT"] = np.ascontiguousarray(x_prompt[c].T)
        m["xTs"] = np.ascontiguousarray(x_sample[b0:b0 + SBATCH].reshape(TS, D).T)
        ck = cache_k[:, b0:b0 + SBATCH]
        m["ckT"] = np.ascontiguousarray(ck.transpose(0, 1, 3, 4, 2))
        cv = cache_v[:, b0:b0 + SBATCH].reshape(2, SBATCH, NG, 4, 128, NH, 128)
        m["cvt"] = np.ascontiguousarray(cv.transpose(0, 1, 5, 2, 4, 3, 6)).reshape(2, SBATCH, NH, NG, 128, 512)
        m["stp"] = np.ascontiguousarray(state_pool[:, b0:b0 + SBATCH].reshape(2, SBATCH, 15, 2, 128).transpose(0, 1, 3, 4, 2))
        m["stc"] = np.ascontiguousarray(state_conv[:, b0:b0 + SBATCH].reshape(2, SBATCH, 2, 2, 128).transpose(0, 1, 3, 4, 2))
        in_maps.append(m)
    res = run_bass_kernel_spmd(nc, in_maps, core_ids=list(range(NCORES)))
    R = res.results
    y_prompt = np.stack([R[c]["oyT"].T for c in range(NCORES)]).astype(f)
    y_sample = np.concatenate([R[c]["oyTs"].T.reshape(SBATCH, SS, D) for c in range(NCORES)]).astype(f)
    nk_p = np.stack([R[c]["okT"].transpose(0, 2, 1).reshape(2, SEQ, NH, 128) for c in range(NCORES)], axis=1).astype(f)
    nv_p = np.stack([R[c]["ov"].reshape(2, SEQ, NH, 128) for c in range(NCORES)], axis=1).astype(f)
    np_p = np.stack([R[c]["opool"].transpose(0, 3, 1, 2).reshape(2, 15, 256) for c in range(NCORES)], axis=1).astype(f)
    nc_p = np.stack([R[c]["oconv"].transpose(0, 3, 1, 2).reshape(2, 2, 256) for c in range(NCORES)], axis=1).astype(f)
    nk_s = np.concatenate([R[c]["okTs"].transpose(0, 2, 1).reshape(2, SBATCH, SS, NH, 128) for c in range(NCORES)], axis=1).astype(f)
    nv_s = np.concatenate([R[c]["ovs"].reshape(2, SBATCH, SS, NH, 128) for c in range(NCORES)], axis=1).astype(f)
    np_s = np.concatenate([R[c]["opools"].transpose(0, 1, 4, 2, 3).reshape(2, SBATCH, 15, 256) for c in range(NCORES)], axis=1).astype(f)
    nc_s = np.concatenate([R[c]["oconvs"].transpose(0, 1, 4, 2, 3).reshape(2, SBATCH, 2, 256) for c in range(NCORES)], axis=1).astype(f)
    return (np.ascontiguousarray(y_prompt), np.ascontiguousarray(y_sample), np.ascontiguousarray(nk_p),
            np.ascontiguousarray(nv_p), np.ascontiguousarray(np_p), np.ascontiguousarray(nc_p),
            np.ascontiguousarray(nk_s), np.ascontiguousarray(nv_s), np.ascontiguousarray(np_s),
            np.ascontiguousarray(nc_s))
```

```python
import math
import os
from contextlib import ExitStack

import numpy as np
import concourse.bass as bass
import concourse.mybir as mybir
from concourse.bass_utils import run_bass_kernel_spmd

F32 = mybir.dt.float32
BF16 = mybir.dt.bfloat16
AF = mybir.ActivationFunctionType
ALU = mybir.AluOpType

NCORES = 8
D = 1024
DFF = 2816
NJ = 22
NH = 4
DEPTH = 2
SS = 32
SBATCH = 2
TS = SS * SBATCH
ALPHA = (2 * DEPTH) ** 0.25
LN_EPS2 = 1e-5 / (ALPHA * ALPHA)
RMS_EPS = 1e-5
ND = 1152
DOFF = 639
NEGM = -30000.0
OC_COLS = [0, 128, 512, 768, 256, 640, 896, 384, 1024, 1152, 1280, 1408, 1536, 1664, 1792, 1920]
OC_KIND = [("u", 0), ("u", 1), ("C", 0), ("h", 0), ("B", 0), ("C", 1), ("h", 1), ("B", 1),
           ("q", 0), ("q", 1), ("q", 2), ("q", 3), ("k", 0), ("k", 1), ("k", 2), ("k", 3)]
POOL_W = (2, 4, 8, 16)
EPOCH = 20000


class Op:
    __slots__ = ("eng", "fn", "deps", "signal", "dma_key", "sig", "idx", "eidx", "group")

    def __init__(self, eng, fn, dma_key):
        self.eng = eng
        self.fn = fn
        self.deps = set()
        self.signal = False
        self.dma_key = dma_key
        self.sig = None
        self.idx = 0
        self.group = False


class Sched:
    ENGS = ("pe", "act", "dve", "pool", "sp")

    def __init__(self, nc):
        self.nc = nc
        self.ops = []
        self.last_writer = {}
        self.readers = {}
        self.eng_count = {}

    def add(self, eng, fn, reads=(), writes=(), dma_key=None, group=False):
        op = Op(eng, fn, dma_key)
        op.group = group
        op.idx = len(self.ops)
        deps = op.deps
        lw = self.last_writer
        rd = self.readers
        eidx = self.eng_count.get(eng, 0)
        op.eidx = eidx
        self.eng_count[eng] = eidx + 1
        is_dma = dma_key is not None

        def consider(p, raw):
            if p is op:
                return
            if p.dma_key is None and not is_dma and p.eng == eng:
                if eng == "pe":
                    return
            deps.add(p)

        for r in reads:
            w = lw.get(r)
            if w is not None:
                consider(w, True)
        for w_ in writes:
            w = lw.get(w_)
            if w is not None:
                consider(w, False)
            for x in rd.get(w_, ()):
                consider(x, False)
        for d in deps:
            d.signal = True
        for r in reads:
            rd.setdefault(r, []).append(op)
        for w_ in writes:
            lw[w_] = op
            rd[w_] = []
        self.ops.append(op)
        return op

    def dma(self, out, in_, reads, writes, key, eng="sp", group=False, **kw):
        return self.add(eng, lambda e: e.dma_start(out=out, in_=in_, **kw), reads, writes,
                        dma_key=key, group=group)

    def emit(self):
        nc = self.nc
        ops = self.ops
        cnt = {e: 0 for e in self.ENGS}
        dcnt = {}
        for op in ops:
            if op.dma_key is not None:
                dcnt[op.dma_key] = dcnt.get(op.dma_key, 0) + 1
                op.sig = ["d", op.dma_key, 16 * dcnt[op.dma_key]]
            elif op.signal:
                c = cnt[op.eng]
                op.sig = ["e", (op.eng, c // EPOCH), (c % EPOCH) + 1]
                cnt[op.eng] = c + 1
        for op in ops:
            if op.dma_key is not None and op.group:
                op.sig[2] = 16 * dcnt[op.dma_key]
                for d in op.deps:
                    assert d.dma_key != op.dma_key, "group DMA depends on its own group: %s" % op.dma_key
        sem_names = sorted({op.sig[1] for op in ops if op.sig is not None}, key=str)
        self.nsems = len(sem_names)
        with ExitStack() as st:
            sems = {}
            for i, k in enumerate(sem_names):
                sems[k] = st.enter_context(nc.semaphore("s%d" % i))
            block = st.enter_context(nc.Block())
            by_eng = {e: [op for op in ops if op.eng == e] for e in self.ENGS}
            final_dma = dict((k, 16 * v) for k, v in dcnt.items())

            def run_engine(e, lst, is_last_waiter):
                waited = {}
                eng_epoch = {}
                for op in lst:
                    best = {}
                    for d in op.deps:
                        kk = (d.sig[0], d.sig[1])
                        if kk not in best or d.sig[2] > best[kk].sig[2]:
                            best[kk] = d
                    for d in sorted(best.values(), key=lambda o: o.idx):
                        kind, key, val = d.sig
                        if kind == "e":
                            pe_, ep = key
                            if eng_epoch.get(pe_, -1) > ep:
                                continue
                            if waited.get(key, 0) >= val:
                                continue
                            waited[key] = val
                            eng_epoch[pe_] = max(eng_epoch.get(pe_, -1), ep)
                        else:
                            if waited.get(key, 0) >= val:
                                continue
                            waited[key] = val
                        e.wait_ge(sems[key], val)
                    ins = op.fn(e)
                    if op.sig is not None:
                        ins.then_inc(sems[op.sig[1]], 16 if op.sig[0] == "d" else 1)
                if is_last_waiter:
                    for k, v in final_dma.items():
                        if waited.get(k, 0) < v:
                            e.wait_ge(sems[k], v)

            block.sync(lambda e: run_engine(e, by_eng["sp"], True))
            if by_eng["pe"]:
                block.tensor(lambda e: run_engine(e, by_eng["pe"], False))
            if by_eng["act"]:
                block.scalar(lambda e: run_engine(e, by_eng["act"], False))
            if by_eng["dve"]:
                block.vector(lambda e: run_engine(e, by_eng["dve"], False))
            if by_eng["pool"]:
                block.gpsimd(lambda e: run_engine(e, by_eng["pool"], False))


def _t5_bucket_np(rel):
    rel = np.asarray(rel, np.int64)
    nb, max_exact = 16, 8
    ret = (rel > 0).astype(np.int64) * nb
    n = np.abs(rel)
    nf = np.maximum(n, 1).astype(np.float32)
    lg = np.log(nf / np.float32(max_exact)).astype(np.float32)
    v = (lg / np.float32(math.log(128 / 8))).astype(np.float32) * np.float32(nb - max_exact)
    large = np.minimum(max_exact + v.astype(np.int32), nb - 1)
    return ret + np.where(n < max_exact, n, large)


def _host_consts():
    d = np.arange(ND) - DOFF
    bk = _t5_bucket_np(d)
    oh = np.zeros((32, ND), np.float32)
    oh[bk, np.arange(ND)] = 1.0
    rc = np.zeros((128, 32), np.float32)
    for i in range(2):
        for p in range(128):
            w = POOL_W[2 * i + (1 if p >= 64 else 0)]
            for t in range(16):
                rc[p, i * 16 + t] = 1.0 / min(w, t + 1)
    ident = np.eye(128, dtype=np.float32)
    return oh, rc, ident


class Tile:
    def __init__(self, kind, idx, T, tok0, segs):
        self.kind = kind
        self.idx = idx
        self.T = T
        self.tok0 = tok0
        self.segs = segs


class _DummySched:
    def __init__(self):
        self.ops = []

    def add(self, *a, **k):
        return None

    def dma(self, *a, **k):
        return None


def build_program(SEQ, PAST, stop=None, wplan=None):
    plan_only = wplan is None
    if plan_only:
        my_plan = []
    NT = SEQ // 512
    NG = PAST // 512
    NKT = SEQ // 128
    nc = bass.Bass("TRN2", target_bir_lowering=False)
    S = _DummySched() if plan_only else Sched(nc)

    def din(name, shape, dt=F32):
        return nc.dram_tensor(name, list(shape), dt, kind="ExternalInput").ap()

    def dout(name, shape):
        return nc.dram_tensor(name, list(shape), F32, kind="ExternalOutput").ap()

    def dscr(name, shape, dt):
        return nc.dram_tensor(name, list(shape), dt, kind="Internal").ap()

    def sb(name, shape, dt):
        return nc.alloc_sbuf_tensor(name, list(shape), dt).ap()

    xT = din("xT", [D, SEQ])
    xTs = din("xTs", [D, TS])
    ckT = din("ckT", [2, SBATCH, NH, 128, PAST])
    cvt = din("cvt", [2, SBATCH, NH, NG, 128, 512])
    stp = din("stp", [2, SBATCH, 2, 128, 15])
    stc = din("stc", [2, SBATCH, 2, 128, 2])
    lng_d = din("lng", [128, 48])
    lnb_d = din("lnb", [128, 48])
    wfi = din("wfi", [2, 2, NJ, 128, 2048])
    wfo = din("wfo", [2, 2, 2, 128, NJ * 512])
    wina = din("wina", [2, 8, 128, 2048])
    winv = din("winv", [2, 128, 4096])
    wout = din("wout", [2, 4, 128, 2048])
    pwbd_d = din("pwbd", [128, 4 * 128])
    pscale_d = din("pscale", [128, 4])
    convw_d = din("convw", [128, 12])
    dlam_d = din("dlam", [1, 512])
    subln_d = din("sublng", [128, 2])
    relb_d = din("relb", [32, 4])
    oh_d = din("ohc", [32, ND])
    rc_d = din("rcfix", [128, 32])
    ident_d = din("identc", [128, 128])

    oyT = dout("oyT", [D, SEQ])
    oyTs = dout("oyTs", [D, TS])
    okT = dout("okT", [2, 512, SEQ])
    ov = dout("ov", [2, SEQ, 512])
    opool = dout("opool", [2, 2, 128, 15])
    oconv = dout("oconv", [2, 2, 128, 2])
    okTs = dout("okTs", [2, 512, TS])
    ovs = dout("ovs", [2, TS, 512])
    opools = dout("opools", [2, SBATCH, 2, 128, 15])
    oconvs = dout("oconvs", [2, SBATCH, 2, 128, 2])

    wfi_s = dscr("wfi_s", [2, 2, NJ, 128, 2048], BF16)
    wfo_s = dscr("wfo_s", [2, 2, 2, 128, NJ * 512], BF16)
    wina_s = dscr("wina_s", [2, 8, 128, 2048], BF16)
    winv_s = dscr("winv_s", [2, 128, 4096], BF16)
    wout_s = dscr("wout_s", [2, 4, 128, 2048], BF16)
    ckT_s = dscr("ckT_s", [2, SBATCH, NH, 128, PAST], BF16)
    cvt_s = dscr("cvt_s", [2, SBATCH, NH, NG, 128, 512], BF16)
    xs_p = dscr("xs_p", [D, SEQ], F32)
    xs_s = dscr("xs_s", [D, TS], F32)
    fvd = dscr("fvd", [4, ND], F32)

    KT = sb("KT", [128, NH, SEQ], BF16)
    VV = sb("VV", [128, NKT, 512], BF16)
    xres = sb("xres", [128, 8, 512], F32)
    xbf = sb("xbf", [128, 8, 512], BF16)
    zsq = sb("zsq", [128, 8, 512], BF16)
    Hb = sb("Hb", [128, NJ * 512], BF16)
    QT = sb("QT", [128, NH, 512], BF16)
    mixT = sb("mixT", [128, 8, 512], BF16)
    BT = sb("BT", [128, 20, 512], BF16)
    NWR = 4
    WR = [sb("WR%d" % i, [128, 2048], BF16) for i in range(NWR)]
    ubuf = sb("ubuf", [128, 2, 528], F32)
    Bbuf = sb("Bbuf", [128, 2, 512], F32)
    Cbuf = sb("Cbuf", [128, 2, 512], F32)
    zbuf = sb("zbuf", [128, 2, 516], F32)
    pa = sb("pa", [128, 528], F32)
    pb = sb("pb", [128, 528], F32)
    dpool = sb("dpool", [128, 2, 512], BF16)
    cacc = sb("cacc", [128, 512], F32)
    kst = [sb("kst%d" % i, [128, 512], F32) for i in range(2)]
    vst = [sb("vst%d" % i, [128, 512], F32) for i in range(2)]
    KTs = sb("KTs", [128, NH, TS], BF16)
    Vs = sb("Vs", [32, SBATCH, 512], BF16)
    ones_ln = sb("ones_ln", [128, 128], BF16)
    ones1 = sb("ones1", [128, 128], BF16)
    ones_rms = sb("ones_rms", [128, 128], BF16)
    ident = sb("ident", [128, 128], BF16)
    PW = sb("PW", [128, 4 * 128], BF16)
    lng = sb("lng_sb", [128, 48], F32)
    lnb = sb("lnb_sb", [128, 48], F32)
    pscale = sb("pscale_sb", [128, 4], F32)
    convw = sb("convw_sb", [128, 12], F32)
    subln = sb("subln_sb", [128, 2], F32)
    rcfix = sb("rcfix_sb", [128, 32], F32)
    lamneg = sb("lamneg", [128, 2], F32)
    smalls = sb("smalls", [128, 16], F32)
    dlb = sb("dlb", [128, 512], F32)
    tmpf = sb("tmpf", [128, 16], F32)

    PSA = nc.alloc_psum_tensor("psall", [128, 8, 512], F32).ap()
    PS = [PSA[:, i, :] for i in range(8)]

    def Hc(j):
        return Hb[:, j * 512:(j + 1) * 512]

    def Hf(j):
        return Hb[:, j * 512:(j + 2) * 512].bitcast(F32)

    def Hr(j, n=1):
        return ["H%d" % (j + i) for i in range(n)]

    PT = [Hb[:, (2 * i) * 512:(2 * i + 2) * 512].rearrange("p (c t) -> p c t", c=2) for i in range(3)]
    PTr = [Hr(2 * i, 2) for i in range(3)]
    R0, R0r = Hf(6), Hr(6, 2)
    R1, R1r = Hf(8), Hr(8, 2)
    O0, O0r = Hf(10), Hr(10, 2)
    O1, O1r = Hf(12), Hr(12, 2)
    RS, RSr = Hf(14), Hr(14, 2)
    SQ, SQr = Hc(16), Hr(16, 1)
    LT0, LT0r = Hf(16), Hr(16, 2)
    LT1, LT1r = Hf(18), Hr(18, 2)
    LT2, LT2r = Hf(20), Hr(20, 2)
    CR = [(Hc(18), Hc(19), Hr(18, 2)), (Hc(20), Hc(21), Hr(20, 2))]

    cast_pieces = {}

    def wfi_group(j):
        p = j // 2
        return 0 if p < 1 else (1 if p < 5 else 2)

    cast_n = [0]

    deferred = []
    defer_mode = [False]

    def cast(dst, src, res, key=None):
        lst = cast_pieces.setdefault(res, [])
        nm = "%s.%d" % (res, len(lst))
        lst.append(nm)

        def emit(extra_reads=()):
            k = cast_n[0] % 2
            cast_n[0] += 1
            S.dma(dst, src, list(extra_reads), [nm, "cwslot%d" % k], "cw%d" % k, eng="pool")

        if defer_mode[0]:
            deferred.append(emit)
        else:
            emit()

    def cast_weights(l):
        for f in range(2):
            if f == 1:
                for i in range(4):
                    cast(wina_s[l, 2 * i:2 * i + 2], wina[l, 2 * i:2 * i + 2], "wina%d" % l)
                cast(winv_s[l], winv[l], "winv%d" % l)
                for i in range(2):
                    cast(wout_s[l, 2 * i:2 * i + 2], wout[l, 2 * i:2 * i + 2], "wout%d" % l)
            for i in range(NJ // 2):
                g = wfi_group(2 * i)
                cast(wfi_s[l, f, 2 * i:2 * i + 2], wfi[l, f, 2 * i:2 * i + 2], "wfi%d%d.%d" % (l, f, g))
            for hf in range(2):
                for i in range(2):
                    cast(wfo_s[l, f, hf][:, i * 5632:(i + 1) * 5632], wfo[l, f, hf][:, i * 5632:(i + 1) * 5632], "wfo%d%d" % (l, f))

    def cast_caches(l):
        for b in range(SBATCH):
            for h in range(NH):
                cast(ckT_s[l, b, h], ckT[l, b, h], "ckT%d%d" % (l, b))
                g0 = 0
                while g0 < NG:
                    g1 = min(NG, g0 + 4)
                    cast(cvt_s[l, b, h, g0:g1], cvt[l, b, h, g0:g1], "cvt%d%d" % (l, b))
                    g0 = g1

    def setup():
        for dst, src, nm in ((lng, lng_d, "lng"), (lnb, lnb_d, "lnb"), (pscale, pscale_d, "pscale"),
                             (convw, convw_d, "convw"), (subln, subln_d, "subln"), (rcfix, rc_d, "rcfix")):
            S.dma(dst, src, [], [nm], "setup", group=True)
        _skip = os.environ.get("K_SKIP", "").split(",")
        if "lam" not in _skip:
            S.dma(dlb, bass.AP(dlam_d.tensor, 0, [[0, 128], [1, 512]]), [], ["dlb"], "setup", group=True)
        S.add("dve", lambda e: e.memset(ones_ln, 1.0 / D), [], ["ones_ln"])
        S.add("dve", lambda e: e.memset(ones1, 1.0), [], ["ones1"])
        S.add("dve", lambda e: e.memset(ones_rms, 1.0 / 128.0), [], ["ones_rms"])
        for l in range(2):
            for k in range(2):
                a = dlb[:, (l * 4 + 2 * k) * 64:(l * 4 + 2 * k + 1) * 64]
                b_ = dlb[:, (l * 4 + 2 * k + 1) * 64:(l * 4 + 2 * k + 2) * 64]
                col = smalls[:, l * 4 + k:l * 4 + k + 1]
                S.add("dve", lambda e, a=a, b_=b_: e.tensor_tensor(pa[:, 0:64], a, b_, ALU.mult),
                      ["dlb"], ["pa"])
                S.add("dve", lambda e, col=col: e.reduce_sum(col, pa[:, 0:64], mybir.AxisListType.X), ["pa"], ["smalls"])
            S.add("act", lambda e, l=l: e.activation(smalls[:, l * 4 + 2:l * 4 + 4], smalls[:, l * 4:l * 4 + 2], AF.Exp),
                  ["smalls"], ["smalls"])
            lam_init = 0.8 - 0.6 * math.exp(-0.3 * l)
            S.add("dve", lambda e, l=l: e.tensor_tensor(smalls[:, 8 + l:9 + l], smalls[:, l * 4 + 3:l * 4 + 4],
                                                        smalls[:, l * 4 + 2:l * 4 + 3], ALU.subtract),
                  ["smalls"], ["smalls"])
            S.add("dve", lambda e, l=l, li=lam_init: e.tensor_scalar(lamneg[:, l:l + 1], smalls[:, 8 + l:9 + l], -li, None, ALU.add),
                  ["smalls"], ["lamneg"])

    def setup_bt():
        pages = []

        def add_pages(buf2d_bf16, ncols_bf16):
            for k in range(ncols_bf16 // 1024):
                pages.append(buf2d_bf16[:, k * 1024:(k + 1) * 1024].bitcast(F32))

        add_pages(KT.rearrange("p h s -> p (h s)"), NH * SEQ)
        add_pages(VV.rearrange("p k c -> p (k c)"), NKT * 512)
        add_pages(mixT.rearrange("p c t -> p (c t)"), 8 * 512)
        add_pages(QT.rearrange("p h t -> p (h t)"), NH * 512)
        relb = pages[0][0:32, 0:4]
        ohs = [pages[1][0:32, :], pages[2][0:32, :], pages[3][0:32, 0:ND - 1024]]
        fvs = [pages[4][0:4, :], pages[5][0:4, :], pages[6][0:4, :]]
        stg = pages[7:27]
        NSTG = len(stg)
        S.dma(relb, relb_d, [], ["pg0"], "setupbt", group=True)
        for i in range(3):
            n = min(512, ND - i * 512)
            S.dma(ohs[i][:, 0:n], oh_d[:, i * 512:i * 512 + n], [], ["pg%d" % (1 + i)], "setupbt", group=True)
        for i in range(3):
            n = min(512, ND - i * 512)
            S.add("pe", lambda e, i=i, n=n: e.matmul(PS[i][0:4, 0:n], relb, ohs[i][:, 0:n], start=True, stop=True),
                  ["pg0", "pg%d" % (1 + i)], ["ps%d" % i])
        S.add("act", lambda e: e.copy(smalls[0:4, 12:13], PS[0][0:4, 0:1]), ["ps0"], ["smalls_c"])
        for i in range(3):
            n = min(512, ND - i * 512)
            S.add("dve", lambda e, i=i, n=n: e.tensor_scalar(fvs[i][:, 0:n], PS[i][0:4, 0:n], smalls[0:4, 12:13], None, ALU.subtract),
                  ["ps%d" % i, "smalls_c"], ["pg%d" % (4 + i)])
            S.dma(fvd[:, i * 512:i * 512 + n], fvs[i][:, 0:n], ["pg%d" % (4 + i)], ["fvd%d" % i], "fvd_w%d" % i)
        k = 0
        for h in range(NH):
            for mi in range(5):
                m = mi - 1
                base = DOFF + 128 * m - 511
                s_ = k % NSTG
                src = bass.AP(fvd.tensor, h * ND + base, [[1, 128], [1, 512]])
                S.dma(stg[s_], src, ["fvd0", "fvd1", "fvd2"], ["hkp%d" % s_], "hk%d" % s_)
                bt = BT[:, h * 5 + mi, :]
                S.add("dve", lambda e, bt=bt, s_=s_: e.tensor_copy(bt, stg[s_][:, ::-1]), ["hkp%d" % s_], ["BT"])
                if m >= 0:
                    if m > 0:
                        S.add("dve", lambda e, h=h, mi=mi, m=m: e.memset(BT[0:64, h * 5 + mi, 0:128 * m], NEGM), [], ["BT"])
                    S.add("dve", lambda e, h=h, mi=mi, m=m: e.memset(BT[64:128, h * 5 + mi, 0:128 * m + 64], NEGM), [], ["BT"])
                k += 1

    ring = [0]
    wl_emitted = [0]
    DEFER_START = 60
    DEFER_EVERY = 3 if SEQ >= 2048 else 1
    LOOK = NWR - 2

    def wsrc(tag):
        kind = tag[0]
        if kind == "wfi":
            _, l, f, j = tag
            return wfi_s[l, f, j], 2048, "wfi%d%d.%d" % (l, f, wfi_group(j))
        if kind == "wfo":
            _, l, f, hf, jg, nj = tag
            return wfo_s[l, f, hf][:, jg * 512:(jg + nj) * 512], nj * 512, "wfo%d%d" % (l, f)
        if kind == "wina":
            _, l, ocp = tag
            return wina_s[l, ocp], 2048, "wina%d" % l
        if kind == "winv":
            _, l, half = tag
            return winv_s[l][:, half * 2048:(half + 1) * 2048], 2048, "winv%d" % l
        if kind == "wout":
            _, l, cp = tag
            return wout_s[l, cp], 2048, "wout%d" % l
        raise ValueError(tag)

    def load_w(tag):
        k = ring[0]
        ring[0] += 1
        if plan_only:
            my_plan.append(tag)
            return WR[k % NWR], "WR%d" % (k % NWR)
        upto = min(len(wplan), k + LOOK + 1)
        while wl_emitted[0] < upto:
            i = wl_emitted[0]
            wl_emitted[0] += 1
            src, ncols, res = wsrc(wplan[i])
            sl = i % NWR
            S.dma(WR[sl][:, 0:ncols], src, cast_pieces[res], ["WR%d" % sl], "WR%d" % sl)
            if deferred and i >= DEFER_START and (i - DEFER_START) % DEFER_EVERY == 0:
                deferred.pop(0)(["WR%d" % sl])
        assert wplan[k] == tag, (wplan[k], tag)
        return WR[k % NWR], "WR%d" % (k % NWR)

    STG = Hb[:, 0:16 * 512].bitcast(F32).rearrange("p (c t) -> p c t", c=8)

    next_x = [None]
    prefetched = set()

    def prefetch_x():
        if next_x[0] is None:
            return
        l2, t2 = next_x[0]
        T2 = t2.T
        for c in range(8):
            if t2.kind == "p":
                src = (xT if l2 == 0 else xs_p)[c * 128:(c + 1) * 128, t2.tok0:t2.tok0 + T2]
                rd = [] if l2 == 0 else ["xsp%d.%d" % (t2.idx, c)]
            else:
                src = (xTs if l2 == 0 else xs_s)[c * 128:(c + 1) * 128, :]
                rd = [] if l2 == 0 else ["xss.%d" % c]
            S.dma(xbf[:, c, :T2], src, rd, ["xbf%d" % c], "xp%d" % c, eng="pool")
        prefetched.add((l2, t2.kind, t2.idx))

    def ln_apply(l, lni, T, final=False):
        for c in range(8):
            S.add("pe", lambda e, c=c: e.matmul(PS[0][:, :T], ones_ln, xbf[:, c, :T], start=(c == 0), stop=(c == 7)),
                  ["xbf%d" % c, "ones_ln"], ["ps0"])
        for c in range(8):
            S.add("pe", lambda e, c=c: e.matmul(PS[1][:, :T], ones_ln, zsq[:, c, :T], start=(c == 0), stop=(c == 7)),
                  ["zsq%d" % c, "ones_ln"], ["ps1"])
        if final:
            prefetch_x()
        mean, m2, rstd = LT0[:, :T], LT1[:, :T], LT2[:, :T]
        S.add("act", lambda e: e.copy(mean, PS[0][:, :T]), ["ps0"], LT0r)
        S.add("act", lambda e: e.activation(m2, PS[0][:, :T], AF.Square), ["ps0"], LT1r)
        S.add("dve", lambda e: e.scalar_tensor_tensor(m2, PS[1][:, :T], LN_EPS2, m2, ALU.add, ALU.subtract), ["ps1"] + LT1r, LT1r)
        S.add("act", lambda e: e.activation(rstd, m2, AF.Ln), LT1r, LT2r)
        S.add("act", lambda e: e.activation(rstd, rstd, AF.Exp, scale=-0.5), LT2r, LT2r)
        for c in range(8):
            gi = (l * 3 + lni) * 8 + c
            xc = xres[:, c, :T]
            xr = "xres%d" % c
            S.add("dve", lambda e, xc=xc: e.tensor_tensor(xc, xc, mean, ALU.subtract), [xr] + LT0r, [xr])
            S.add("dve", lambda e, xc=xc: e.tensor_tensor(xc, xc, rstd, ALU.mult), [xr] + LT2r, [xr])
            if final:
                S.add("act", lambda e, xc=xc, c=c, gi=gi: e.activation(STG[:, c, :T], xc, AF.Identity, scale=lng[:, gi:gi + 1], bias=lnb[:, gi:gi + 1]),
                      [xr, "lng", "lnb"], Hr(2 * c, 2))
                continue
            S.add("act", lambda e, xc=xc, c=c, gi=gi: e.activation(xbf[:, c, :T], xc, AF.Identity, scale=lng[:, gi:gi + 1], bias=lnb[:, gi:gi + 1]),
                  [xr, "lng", "lnb"], ["xbf%d" % c])
        for c in range(8):
            if final:
                break
            gi = (l * 3 + lni) * 8 + c
            xc = xres[:, c, :T]
            xr = "xres%d" % c
            if c < 4:
                S.add("act", lambda e, xc=xc, gi=gi: e.activation(xc, xc, AF.Identity, scale=lng[:, gi:gi + 1], bias=lnb[:, gi:gi + 1]),
                      [xr, "lng", "lnb"], [xr])
            else:
                S.add("dve", lambda e, xc=xc, gi=gi: e.tensor_scalar(xc, xc, lng[:, gi:gi + 1], lnb[:, gi:gi + 1], ALU.mult, ALU.add),
                      [xr, "lng", "lnb"], [xr])

    def residual_prep(c, bank, T, coef):
        xc = xres[:, c, :T]
        S.add("dve", lambda e: e.scalar_tensor_tensor(xc, PS[bank][:, :T], coef, xc, ALU.mult, ALU.add),
              ["ps%d" % bank, "xres%d" % c], ["xres%d" % c])
        S.add("dve", lambda e: e.tensor_copy(xbf[:, c, :T], xc), ["xres%d" % c], ["xbf%d" % c])
        S.add("act", lambda e: e.activation(zsq[:, c, :T], xc, AF.Square), ["xres%d" % c], ["zsq%d" % c])

    def ffn_ln(l, f, lni, T, final=False, mid_hook=None):
        wres = "wfi%d%d" % (l, f)
        for j in range(NJ):
            w, wr = load_w(("wfi", l, f, j))
            bg, bu = 2 * (j % 2), 2 * (j % 2) + 1
            for gu, bank in ((0, bg), (1, bu)):
                for kc in range(8):
                    S.add("pe", lambda e, w=w, gu=gu, kc=kc, bank=bank: e.matmul(
                        PS[bank][:, :T], w[:, (gu * 8 + kc) * 128:(gu * 8 + kc + 1) * 128], xbf[:, kc, :T],
                        start=(kc == 0), stop=(kc == 7)), [wr, "xbf%d" % kc], ["ps%d" % bank])
            hj = Hc(j)[:, :T]
            S.add("act", lambda e, hj=hj, bg=bg: e.activation(hj, PS[bg][:, :T], AF.Silu), ["ps%d" % bg], Hr(j))
            S.add("dve", lambda e, hj=hj, bu=bu: e.tensor_tensor(hj, hj, PS[bu][:, :T], ALU.mult), ["ps%d" % bu] + Hr(j), Hr(j))
        if mid_hook is not None:
            mid_hook()
        wres = "wfo%d%d" % (l, f)
        for hf in range(2):
            for jg in range(0, NJ, 4):
                nj = min(4, NJ - jg)
                w, wr = load_w(("wfo", l, f, hf, jg, nj))
                for jj in range(nj):
                    j = jg + jj
                    for cc in range(4):
                        S.add("pe", lambda e, w=w, jj=jj, cc=cc, j=j: e.matmul(
                            PS[4 + cc][:, :T], w[:, jj * 512 + cc * 128:jj * 512 + (cc + 1) * 128], Hc(j)[:, :T],
                            start=(j == 0), stop=(j == NJ - 1)), [wr] + Hr(j), ["ps%d" % (4 + cc)])
            for cc in range(4):
                residual_prep(hf * 4 + cc, 4 + cc, T, 0.5 / ALPHA)
        ln_apply(l, lni, T, final)

    def in_proj(l, tile):
        T = tile.T
        g_ = ["BT"] if (l == 0 and tile.kind == "p" and tile.idx == 0) else []
        bank_i = [0]
        kcount = [0]
        for ocp in range(8):
            w, wr = load_w(("wina", l, ocp))
            for o2 in range(2):
                oc = ocp * 2 + o2
                bank = bank_i[0] % 4
                bank_i[0] += 1
                for kc in range(8):
                    S.add("pe", lambda e, w=w, o2=o2, kc=kc, bank=bank: e.matmul(
                        PS[bank][:, :T], w[:, (o2 * 8 + kc) * 128:(o2 * 8 + kc + 1) * 128], xbf[:, kc, :T],
                        start=(kc == 0), stop=(kc == 7)), [wr, "xbf%d" % kc], ["ps%d" % bank])
                kind, i = OC_KIND[oc]
                psb = PS[bank]
                pr = "ps%d" % bank
                if kind == "u":
                    for s, (c0, n) in enumerate(tile.segs):
                        ub = s * (16 + n)
                        S.add("act", lambda e, i=i, ub=ub, c0=c0, n=n, psb=psb: e.copy(ubuf[:, i, ub + 16:ub + 16 + n], psb[:, c0:c0 + n]),
                              [pr], ["ubuf%d" % i])
                elif kind == "C":
                    S.add("act", lambda e, i=i, psb=psb: e.copy(Cbuf[:, i, :T], psb[:, :T]), [pr] + g_, ["Cbuf%d" % i])
                elif kind == "B":
                    S.add("act", lambda e, i=i, psb=psb: e.copy(Bbuf[:, i, :T], psb[:, :T]), [pr] + g_, ["Bbuf%d" % i])
                elif kind == "h":
                    for s, (c0, n) in enumerate(tile.segs):
                        zb_ = s * (2 + n)
                        S.add("dve", lambda e, i=i, zb_=zb_, c0=c0, n=n, psb=psb: e.tensor_tensor(
                            zbuf[:, i, zb_ + 2:zb_ + 2 + n], psb[:, c0:c0 + n], Cbuf[:, i, c0:c0 + n], ALU.mult),
                            [pr, "Cbuf%d" % i], ["zbuf%d" % i])
                elif kind == "q":
                    S.add("act", lambda e, i=i, psb=psb: e.activation(QT[:, i, :T], psb[:, :T], AF.Identity, scale=0.125), [pr] + g_, ["QT%d" % i])
                elif kind == "k":
                    ks = kcount[0] % 2
                    kcount[0] += 1
                    S.add("act", lambda e, ks=ks, psb=psb: e.copy(kst[ks][:, :T], psb[:, :T]), [pr] + g_, ["kst%d" % ks])
                    if tile.kind == "p":
                        S.add("dve", lambda e, i=i, ks=ks: e.tensor_copy(KT[:, i, tile.tok0:tile.tok0 + T], kst[ks][:, :T]),
                              ["kst%d" % ks], ["KT%d.%d" % (i, tile.idx)])
                        S.dma(okT[l, i * 128:(i + 1) * 128, tile.tok0:tile.tok0 + T], kst[ks][:, :T], ["kst%d" % ks], [], "kst%d" % ks)
                    else:
                        S.add("dve", lambda e, i=i, ks=ks: e.tensor_copy(KTs[:, i, :T], kst[ks][:, :T]), ["kst%d" % ks], ["KTs%d" % i])
                        S.dma(okTs[l, i * 128:(i + 1) * 128, :], kst[ks][:, :T], ["kst%d" % ks], [], "kst%d" % ks)
        wv = []
        for half in range(2):
            w, wr = load_w(("winv", l, half))
            wv.append((w, wr))
        if tile.kind == "p":
            subs = [(ts * 128, 128) for ts in range(T // 128)]
        else:
            subs = [(b * SS, SS) for b in range(SBATCH)]
        for si, (t0, n) in enumerate(subs):
            bank = 4 + (si % 4)
            for kc in range(8):
                w, wr = wv[kc // 4]
                S.add("pe", lambda e, w=w, kc=kc, bank=bank, t0=t0, n=n: e.matmul(
                    PS[bank][0:n, :], xbf[:, kc, t0:t0 + n], w[:, (kc % 4) * 512:(kc % 4 + 1) * 512],
                    start=(kc == 0), stop=(kc == 7)), [wr, "xbf%d" % kc], ["ps%d" % bank])
            vs_ = si % 2
            S.add("act", lambda e, vs_=vs_, bank=bank, n=n: e.copy(vst[vs_][0:n, :], PS[bank][0:n, :]), ["ps%d" % bank] + g_, ["vst%d" % vs_])
            if tile.kind == "p":
                kt = (tile.tok0 + t0) // 128
                S.add("dve", lambda e, kt=kt, vs_=vs_: e.tensor_copy(VV[:, kt, :], vst[vs_][:, :]), ["vst%d" % vs_], ["VV%d" % kt])
                S.dma(ov[l, tile.tok0 + t0:tile.tok0 + t0 + n, :], vst[vs_][0:n, :], ["vst%d" % vs_], [], "vst%d" % vs_)
            else:
                S.add("dve", lambda e, si=si, vs_=vs_, n=n: e.tensor_copy(Vs[0:n, si, :], vst[vs_][0:n, :]), ["vst%d" % vs_], ["Vs%d" % si])
                S.dma(ovs[l, t0:t0 + n, :], vst[vs_][0:n, :], ["vst%d" % vs_], [], "vst%d" % vs_)

    mix_post = []

    def mixers(l, tile, last_prompt):
        T = tile.T
        for i in range(2):
            ur = "ubuf%d" % i
            if tile.kind == "p" and tile.idx == 0:
                S.add("dve", lambda e, i=i: e.memset(ubuf[:, i, 0:16], 0.0), [], [ur])
            if tile.kind == "s":
                for s, (c0, n) in enumerate(tile.segs):
                    ub = s * (16 + n)
                    S.dma(ubuf[:, i, ub + 1:ub + 16], stp[l, s, i], [], [ur], "hl%d" % (s * 2 + i),
                          allow_slow_non_contiguous=True)
            wa, wb = POOL_W[2 * i], POOL_W[2 * i + 1]
            for s, (c0, n) in enumerate(tile.segs):
                ub = s * (16 + n)
                L = 15 + n

                def E(a, b, ub=ub, i=i):
                    return ubuf[:, i, ub + 1 + a:ub + 1 + b]

                S.add("dve", lambda e, E=E, L=L: e.tensor_tensor(pa[:, 1:L], E(1, L), E(0, L - 1), ALU.add), [ur], ["pa"])
                S.add("dve", lambda e, L=L: e.tensor_tensor(pb[:, 3:L], pa[:, 3:L], pa[:, 1:L - 2], ALU.add), ["pa"], ["pb"])
                if i == 1:
                    S.add("dve", lambda e, L=L: e.tensor_tensor(pa[:, 7:L], pb[:, 7:L], pb[:, 3:L - 4], ALU.add), ["pb"], ["pa"])
                    S.add("dve", lambda e, L=L: e.tensor_tensor(pb[:, 15:L], pa[:, 15:L], pa[:, 7:L - 8], ALU.add), ["pa"], ["pb"])
                S.add("dve", lambda e, E=E, L=L, i=i, c0=c0, n=n, wa=wa: e.scalar_tensor_tensor(
                    dpool[0:64, i, c0:c0 + n], pa[0:64, 15:L], 1.0 / wa, E(15, L)[0:64], ALU.mult, ALU.subtract),
                    ["pa", ur], ["dpool%d" % i])
                S.add("dve", lambda e, E=E, L=L, i=i, c0=c0, n=n, wb=wb: e.scalar_tensor_tensor(
                    dpool[64:128, i, c0:c0 + n], pb[64:128, 15:L], 1.0 / wb, E(15, L)[64:128], ALU.mult, ALU.subtract),
                    ["pb", ur], ["dpool%d" % i])
                if tile.kind == "p" and tile.idx == 0:
                    for (lo, hi, src) in ((0, 64, pa), (64, 128, pb)):
                        S.add("dve", lambda e, lo=lo, hi=hi, src=src, i=i: e.tensor_tensor(
                            tmpf[lo:hi, 0:15], src[lo:hi, 15:30], rcfix[lo:hi, i * 16:i * 16 + 15], ALU.mult),
                            ["pa", "pb", "rcfix"], ["tmpf"])
                        S.add("dve", lambda e, lo=lo, hi=hi, E=E, i=i: e.tensor_tensor(
                            dpool[lo:hi, i, 0:15], tmpf[lo:hi, 0:15], E(15, 30)[lo:hi], ALU.subtract),
                            ["tmpf", ur], ["dpool%d" % i])
                if tile.kind == "s":
                    S.dma(opools[l, s, i], E(n, n + 15), [ur], [], "so%d" % (s * 2 + i),
                          allow_slow_non_contiguous=True)
                elif last_prompt:
                    S.dma(opool[l, i], E(n, n + 15), [ur], [], "so%d" % (s * 2 + i),
                          allow_slow_non_contiguous=True)
                else:
                    S.add("dve", lambda e, E=E, n=n: e.tensor_copy(tmpf[:, 0:15], E(n, n + 15)), [ur], ["tmpf"])
                    S.add("dve", lambda e, E=E: e.tensor_copy(E(0, 15), tmpf[:, 0:15]), ["tmpf"], [ur])

            def post(i=i, bank=i):
                S.add("pe", lambda e: e.matmul(PS[bank][:, :T], PW[:, (l * 2 + i) * 128:(l * 2 + i + 1) * 128], dpool[:, i, :T],
                                               start=True, stop=True), ["PW", "dpool%d" % i], ["ps%d" % bank])
                S.add("act", lambda e: e.activation(mixT[:, i, :T], PS[bank][:, :T], AF.Identity, scale=pscale[:, l * 2 + i:l * 2 + i + 1]),
                      ["ps%d" % bank, "pscale"], ["mix%d" % i])

            mix_post.append(post)
        for i in range(2):
            zr = "zbuf%d" % i
            if tile.kind == "p" and tile.idx == 0:
                S.add("dve", lambda e, i=i: e.memset(zbuf[:, i, 0:2], 0.0), [], [zr])
            if tile.kind == "s":
                for s, (c0, n) in enumerate(tile.segs):
                    zb_ = s * (2 + n)
                    S.dma(zbuf[:, i, zb_:zb_ + 2], stc[l, s, i], [], [zr], "hl%d" % (4 + s * 2 + i),
                          allow_slow_non_contiguous=True)

            def wi(j, i=i):
                k = (l * 3 + j) * 2 + i
                return convw[:, k:k + 1]

            for s, (c0, n) in enumerate(tile.segs):
                zb_ = s * (2 + n)

                def Z(a, b, zb_=zb_, i=i):
                    return zbuf[:, i, zb_ + a:zb_ + b]

                acc = cacc[:, c0:c0 + n]
                S.add("dve", lambda e, Z=Z, n=n, acc=acc, wi=wi: e.tensor_scalar(acc, Z(2, 2 + n), wi(2), None, ALU.mult), [zr, "convw"], ["cacc"])
                S.add("dve", lambda e, Z=Z, n=n, acc=acc, wi=wi: e.scalar_tensor_tensor(acc, Z(1, 1 + n), wi(1), acc, ALU.mult, ALU.add),
                      [zr, "convw", "cacc"], ["cacc"])
                S.add("dve", lambda e, Z=Z, n=n, acc=acc, wi=wi: e.scalar_tensor_tensor(acc, Z(0, n), wi(0), acc, ALU.mult, ALU.add),
                      [zr, "convw", "cacc"], ["cacc"])
                if tile.kind == "s":
                    S.dma(oconvs[l, s, i], Z(n, n + 2), [zr], [], "so%d" % (4 + s * 2 + i), allow_slow_non_contiguous=True)
                elif last_prompt:
                    S.dma(oconv[l, i], Z(n, n + 2), [zr], [], "so%d" % (4 + s * 2 + i), allow_slow_non_contiguous=True)
                else:
                    S.add("dve", lambda e, Z=Z, n=n: e.tensor_copy(tmpf[:, 0:2], Z(n, n + 2)), [zr], ["tmpf"])
                    S.add("dve", lambda e, Z=Z: e.tensor_copy(Z(0, 2), tmpf[:, 0:2]), ["tmpf"], [zr])
            S.add("dve", lambda e, i=i: e.tensor_tensor(mixT[:, 2 + i, :T], cacc[:, :T], Bbuf[:, i, :T], ALU.mult),
                  ["cacc", "Bbuf%d" % i], ["mix%d" % (2 + i)])

    def attention(l, q0, Tq, ktiles_for_head, mix_c0):
        lam_init = 0.8 - 0.6 * math.exp(-0.3 * l)
        seq = []
        for h in range(NH):
            kts = ktiles_for_head(h)
            for idx, kt in enumerate(kts):
                seq.append((h, idx, len(kts), kt))
        cnt = [0]

        def emit_qk(item, k):
            h, idx, nk_t, kt = item
            if "load" in kt:
                kt["load"]()
            sset = k % 2
            nk = kt["nk"]
            for c in range(2):
                bank = 2 * sset + c
                bias = kt["bias"]
                S.add("pe", lambda e, kt=kt, c=c, bank=bank, nk=nk, h=h, bias=bias: e.matmul(
                    PS[bank][0:nk, :Tq], kt["kt"](c), QT[64 * c:64 * c + 64, h, q0:q0 + Tq], start=True, stop=(bias is None)),
                    kt["res"] + ["QT%d" % h], ["ps%d" % bank])
                if bias is not None:
                    S.add("pe", lambda e, bank=bank, nk=nk, bias=bias: e.matmul(
                        PS[bank][0:nk, :Tq], ident[0:nk, 0:nk], bias, start=False, stop=True),
                        ["ident", "BT"], ["ps%d" % bank])
            pt = k % 3
            S.add("act", lambda e, pt=pt, sset=sset, nk=nk: e.activation(PT[pt][0:nk, :, :Tq], PSA[0:nk, 2 * sset:2 * sset + 2, :Tq], AF.Exp),
                  ["ps%d" % (2 * sset), "ps%d" % (2 * sset + 1)], list(PTr[pt]))

        def emit_pv(item, k):
            h, idx, nk_t, kt = item
            nk = kt["nk"]
            pt = k % 3
            for c in range(2):
                S.add("pe", lambda e, kt=kt, c=c, nk=nk, pt=pt, idx=idx, nk_t=nk_t: e.matmul(
                    PS[4 + c][:, :Tq], kt["v"], PT[pt][0:nk, c, :Tq], start=(idx == 0), stop=(idx == nk_t - 1)),
                    kt["res"] + [PTr[pt][c]], ["ps%d" % (4 + c)])
                S.add("pe", lambda e, c=c, nk=nk, pt=pt, idx=idx, nk_t=nk_t: e.matmul(
                    PS[6 + c][:, :Tq], ones1[0:nk, :], PT[pt][0:nk, c, :Tq], start=(idx == 0), stop=(idx == nk_t - 1)),
                    ["ones1", PTr[pt][c]], ["ps%d" % (6 + c)])
            if idx == nk_t - 1:
                finalize(h)

        def finalize(h):
            r0, r1, o0, o1, rs, sq = R0[:, :Tq], R1[:, :Tq], O0[:, :Tq], O1[:, :Tq], RS[:, :Tq], SQ[:, :Tq]
            S.add("dve", lambda e: e.tensor_copy(o0, PS[4][:, :Tq]), ["ps4"], O0r)
            S.add("dve", lambda e: e.tensor_copy(o1, PS[5][:, :Tq]), ["ps5"], O1r)
            S.add("act", lambda e: e.activation(r0, PS[6][:, :Tq], AF.Ln), ["ps6"], R0r)
            S.add("act", lambda e: e.activation(r1, PS[7][:, :Tq], AF.Ln), ["ps7"], R1r)
            S.add("act", lambda e: e.activation(r0, r0, AF.Exp, scale=-1.0), R0r, R0r)
            S.add("act", lambda e: e.activation(r1, r1, AF.Exp, scale=-1.0), R1r, R1r)
            S.add("dve", lambda e: e.tensor_tensor(o0, o0, r0, ALU.mult), O0r + R0r, O0r)
            S.add("dve", lambda e: e.tensor_tensor(o1, o1, r1, ALU.mult), O1r + R1r, O1r)
            S.add("dve", lambda e: e.scalar_tensor_tensor(o0, o1, lamneg[:, l:l + 1], o0, ALU.mult, ALU.add),
                  O0r + O1r + ["lamneg"], O0r)
            S.add("act", lambda e: e.activation(sq, o0, AF.Square), O0r, SQr)
            k = cnt[0]
            cnt[0] += 1
            bank = 2 * (k % 2)
            S.add("pe", lambda e: e.matmul(PS[bank][:, :Tq], ones_rms, sq, start=True, stop=True), SQr + ["ones_rms"], ["ps%d" % bank])
            S.add("act", lambda e: e.activation(rs, PS[bank][:, :Tq], AF.Ln, bias=smalls[:, 13:14]), ["ps%d" % bank, "epsb"], RSr)
            S.add("act", lambda e: e.activation(rs, rs, AF.Exp, scale=-0.5), RSr, RSr)
            S.add("dve", lambda e: e.tensor_tensor(o0, o0, rs, ALU.mult), O0r + RSr, O0r)
            S.add("dve", lambda e: e.tensor_scalar(mixT[:, 4 + h, mix_c0:mix_c0 + Tq], o0, subln[:, l:l + 1], 1.0 - lam_init, ALU.mult, ALU.mult),
                  O0r + ["subln"], ["mix%d" % (4 + h)])

        prev = None
        for item in seq:
            k = cnt[0]
            cnt[0] += 1
            emit_qk(item, k)
            if prev is not None:
                emit_pv(*prev)
            prev = (item, k)
        emit_pv(*prev)

    def prompt_attention(l, tile):
        I = tile.idx

        def kts(h):
            out = []
            for j in range(4 * I + 4):
                m = j - 4 * I
                bias = BT[:, h * 5 + (m + 1), :tile.T] if m >= -1 else None
                out.append(dict(
                    kt=(lambda c, j=j, h=h: KT[64 * c:64 * c + 64, h, j * 128:(j + 1) * 128]),
                    v=VV[:, j, h * 128:(h + 1) * 128], nk=128, bias=bias,
                    res=["KT%d.%d" % (h, j // 4), "VV%d" % j]))
            return out

        attention(l, 0, tile.T, kts, 0)

    def sample_attention(l, tile):
        crc = [0]
        for b in range(SBATCH):
            def kts(h, b=b):
                out = []
                for g in range(NG):
                    slot = crc[0] % 2
                    crc[0] += 1
                    kbuf, vbuf, rr = CR[slot]

                    def load(g=g, h=h, kbuf=kbuf, vbuf=vbuf, rr=rr, slot=slot):
                        S.dma(kbuf, ckT_s[l, b, h][:, g * 512:(g + 1) * 512], cast_pieces["ckT%d%d" % (l, b)], [rr[0]], "crk%d" % slot)
                        S.dma(vbuf, cvt_s[l, b, h, g], cast_pieces["cvt%d%d" % (l, b)], [rr[1]], "crv%d" % slot)

                    for t in range(4):
                        last = (g == NG - 1 and t == 3)
                        d = dict(
                            kt=(lambda c, kbuf=kbuf, t=t: kbuf[64 * c:64 * c + 64, t * 128:(t + 1) * 128]),
                            v=vbuf[:, t * 128:(t + 1) * 128], nk=128,
                            bias=(BT[:, h * 5 + 0, 0:SS] if last else None), res=list(rr))
                        if t == 0:
                            d["load"] = load
                        out.append(d)
                out.append(dict(
                    kt=(lambda c, h=h: KTs[64 * c:64 * c + 64, h, b * SS:(b + 1) * SS]),
                    v=Vs[0:SS, b, h * 128:(h + 1) * 128], nk=SS,
                    bias=BT[0:SS, h * 5 + 1, 0:SS], res=["KTs%d" % h, "Vs%d" % b]))
                return out

            attention(l, b * SS, SS, kts, b * SS)

    def out_proj(l, T):
        bank_i = 0
        for cp in range(4):
            w, wr = load_w(("wout", l, cp))
            for c2 in range(2):
                c = cp * 2 + c2
                bank = bank_i % 4
                bank_i += 1
                for kc in range(8):
                    S.add("pe", lambda e, w=w, c2=c2, kc=kc, bank=bank: e.matmul(
                        PS[bank][:, :T], w[:, (c2 * 8 + kc) * 128:(c2 * 8 + kc + 1) * 128], mixT[:, kc, :T],
                        start=(kc == 0), stop=(kc == 7)), [wr, "mix%d" % kc], ["ps%d" % bank])
                residual_prep(c, bank, T, 1.0 / ALPHA)
        ln_apply(l, 1, T)

    def load_x(l, tile):
        T = tile.T
        for c in range(8):
            if tile.kind == "p":
                src = (xT if l == 0 else xs_p)[c * 128:(c + 1) * 128, tile.tok0:tile.tok0 + T]
                rd = [] if l == 0 else ["xsp%d.%d" % (tile.idx, c)]
            else:
                src = (xTs if l == 0 else xs_s)[c * 128:(c + 1) * 128, :]
                rd = [] if l == 0 else ["xss.%d" % c]
            S.dma(xres[:, c, :T], src, rd, ["xres%d" % c], "xl%d" % c)
            if (l, tile.kind, tile.idx) not in prefetched:
                S.add("dve", lambda e, c=c: e.tensor_copy(xbf[:, c, :T], xres[:, c, :T]), ["xres%d" % c], ["xbf%d" % c])

    def store_x(l, tile):
        T = tile.T
        for c in range(8):
            if tile.kind == "p":
                dst = (xs_p if l == 0 else oyT)[c * 128:(c + 1) * 128, tile.tok0:tile.tok0 + T]
                wr = ["xsp%d.%d" % (tile.idx, c)] if l == 0 else []
            else:
                dst = (xs_s if l == 0 else oyTs)[c * 128:(c + 1) * 128, :]
                wr = ["xss.%d" % c] if l == 0 else []
            S.dma(dst, STG[:, c, :T], Hr(2 * c, 2), wr, "xst%d" % c)

    tiles = [Tile("p", i, 512, i * 512, [(0, 512)]) for i in range(NT)]
    tiles.append(Tile("s", NT, TS, 0, [(b * SS, SS) for b in range(SBATCH)]))

    S.dma(ident, ident_d, [], ["ident"], "c_small", eng="pool", group=True)
    S.dma(PW, pwbd_d, [], ["PW"], "c_small", eng="pool", group=True)
    cast_weights(0)
    defer_mode[0] = True
    cast_caches(0)
    cast_weights(1)
    cast_caches(1)
    first = [True]
    for l in range(DEPTH):
        for ti, tile in enumerate(tiles):
            if stop is not None and (l, ti) == stop[0:2]:
                phases = stop[2]
            else:
                phases = 99
                if stop is not None and (l, ti) > stop[0:2]:
                    continue
            load_x(l, tile)
            hook = None
            if first[0]:
                first[0] = False
                setup()
                S.add("dve", lambda e: e.memset(smalls[:, 13:14], RMS_EPS), [], ["epsb"])
                hook = setup_bt
                if phases < 1:
                    setup_bt()
            if phases >= 1:
                ffn_ln(l, 0, 0, tile.T, mid_hook=hook)
            if phases >= 2:
                in_proj(l, tile)
            if phases >= 3:
                mixers(l, tile, tile.kind == "p" and tile.idx == NT - 1)
            if phases >= 4:
                if tile.kind == "p":
                    prompt_attention(l, tile)
                else:
                    sample_attention(l, tile)
            for p_ in mix_post:
                p_()
            del mix_post[:]
            if phases >= 5:
                out_proj(l, tile.T)
            next_x[0] = None
            if stop is None:
                if ti + 1 < len(tiles):
                    next_x[0] = (l, tiles[ti + 1])
                elif l + 1 < DEPTH:
                    next_x[0] = (l + 1, tiles[0])
            if phases >= 6:
                ffn_ln(l, 1, 2, tile.T, final=True)
            store_x(l, tile)
    if plan_only:
        return build_program(SEQ, PAST, stop, wplan=my_plan)
    assert not deferred or stop is not None, "deferred casts left: %d" % len(deferred)
    S.emit()
    return nc, S


_PROG_CACHE = {}
_DEBUG_STOP = None


def _prep_shared(ln_g, ln_b, w_ffn_in, w_ffn_out, w_in, w_out, pool_w, pool_scale, conv_w,
                 diff_lambda, subln_g, rel_bias):
    f = np.float32
    W = np.asarray(w_ffn_in, f).reshape(2, 2, 8, 128, 2, NJ, 128)
    wfi = np.ascontiguousarray(W.transpose(0, 1, 5, 3, 4, 2, 6)).reshape(2, 2, NJ, 128, 2048)
    W = np.asarray(w_ffn_out, f).reshape(2, 2, NJ, 128, 2, 512)
    wfo = np.ascontiguousarray(W.transpose(0, 1, 4, 3, 2, 5)).reshape(2, 2, 2, 128, NJ * 512)
    Wi = np.asarray(w_in, f).reshape(2, 8, 128, 2560)
    wa = np.stack([Wi[:, :, :, c0:c0 + 128] for c0 in OC_COLS], axis=1)
    wa = wa.reshape(2, 8, 2, 8, 128, 128).transpose(0, 1, 4, 2, 3, 5)
    wina = np.ascontiguousarray(wa).reshape(2, 8, 128, 2048)
    winv = np.ascontiguousarray(Wi[:, :, :, 2048:2560].transpose(0, 2, 1, 3)).reshape(2, 128, 4096)
    Wo = np.asarray(w_out, f).reshape(2, 8, 128, 4, 2, 128)
    wout = np.ascontiguousarray(Wo.transpose(0, 3, 2, 4, 1, 5)).reshape(2, 4, 128, 2048)
    lng = np.ascontiguousarray(np.asarray(ln_g, f).reshape(2, 3, 8, 128).transpose(3, 0, 1, 2)).reshape(128, 48)
    lnb = np.ascontiguousarray(np.asarray(ln_b, f).reshape(2, 3, 8, 128).transpose(3, 0, 1, 2)).reshape(128, 48)
    pw = np.asarray(pool_w, f)
    pwbd = np.zeros((2, 2, 128, 128), f)
    for l in range(2):
        for i in range(2):
            pwbd[l, i, 0:64, 0:64] = pw[l, 2 * i]
            pwbd[l, i, 64:128, 64:128] = pw[l, 2 * i + 1]
    pwbd = np.ascontiguousarray(pwbd.transpose(2, 0, 1, 3)).reshape(128, 512)
    pscale = np.ascontiguousarray(np.asarray(pool_scale, f).reshape(2, 2, 128).transpose(2, 0, 1)).reshape(128, 4)
    convw = np.ascontiguousarray(np.asarray(conv_w, f).reshape(2, 3, 2, 128).transpose(3, 0, 1, 2)).reshape(128, 12)
    dlam = np.ascontiguousarray(np.asarray(diff_lambda, f)).reshape(1, 512)
    sublng = np.ascontiguousarray(np.asarray(subln_g, f).T)
    relb = np.ascontiguousarray(np.asarray(rel_bias, f))
    oh, rc, ident = _host_consts()
    return dict(wfi=wfi, wfo=wfo, wina=wina, winv=winv, wout=wout, lng=lng, lnb=lnb, pwbd=pwbd,
                pscale=pscale, convw=convw, dlam=dlam, sublng=sublng, relb=relb, ohc=oh, rcfix=rc, identc=ident)


def kernel(x_prompt, x_sample, cache_k, cache_v, state_pool, state_conv,
           ln_g, ln_b, w_ffn_in, w_ffn_out, w_in, w_out,
           pool_w, pool_scale, conv_w, diff_lambda, subln_g, rel_bias):
    f = np.float32
    x_prompt = np.asarray(x_prompt, f)
    x_sample = np.asarray(x_sample, f)
    cache_k = np.asarray(cache_k, f)
    cache_v = np.asarray(cache_v, f)
    state_pool = np.asarray(state_pool, f)
    state_conv = np.asarray(state_conv, f)
    BATCH, SEQ = x_prompt.shape[:2]
    PAST = cache_k.shape[2]
    NG = PAST // 512
    assert BATCH == NCORES and x_sample.shape[0] == NCORES * SBATCH and x_sample.shape[1] == SS
    key = (SEQ, PAST)
    if key not in _PROG_CACHE:
        _PROG_CACHE[key] = build_program(SEQ, PAST, _DEBUG_STOP)[0]
    nc = _PROG_CACHE[key]
    shared = _prep_shared(ln_g, ln_b, w_ffn_in, w_ffn_out, w_in, w_out, pool_w, pool_scale, conv_w,
                          diff_lambda, subln_g, rel_bias)
    in_maps = []
    for c in range(NCORES):
        b0 = c * SBATCH
        m = dict(shared)
        m["xT"] = np.ascontiguousarray(x_prompt[c].T)
        m["xTs"] = np.ascontiguousarray(x_sample[b0:b0 + SBATCH].reshape(TS, D).T)
        ck = cache_k[:, b0:b0 + SBATCH]
        m["ckT"] = np.ascontiguousarray(ck.transpose(0, 1, 3, 4, 2))
        cv = cache_v[:, b0:b0 + SBATCH].reshape(2, SBATCH, NG, 4, 128, NH, 128)
        m["cvt"] = np.ascontiguousarray(cv.transpose(0, 1, 5, 2, 4, 3, 6)).reshape(2, SBATCH, NH, NG, 128, 512)
        m["stp"] = np.ascontiguousarray(state_pool[:, b0:b0 + SBATCH].reshape(2, SBATCH, 15, 2, 128).transpose(0, 1, 3, 4, 2))
        m["stc"] = np.ascontiguousarray(state_conv[:, b0:b0 + SBATCH].reshape(2, SBATCH, 2, 2, 128).transpose(0, 1, 3, 4, 2))
        in_maps.append(m)
    res = run_bass_kernel_spmd(nc, in_maps, core_ids=list(range(NCORES)))
    R = res.results
    y_prompt = np.stack([R[c]["oyT"].T for c in range(NCORES)]).astype(f)
    y_sample = np.concatenate([R[c]["oyTs"].T.reshape(SBATCH, SS, D) for c in range(NCORES)]).astype(f)
    nk_p = np.stack([R[c]["okT"].transpose(0, 2, 1).reshape(2, SEQ, NH, 128) for c in range(NCORES)], axis=1).astype(f)
    nv_p = np.stack([R[c]["ov"].reshape(2, SEQ, NH, 128) for c in range(NCORES)], axis=1).astype(f)
    np_p = np.stack([R[c]["opool"].transpose(0, 3, 1, 2).reshape(2, 15, 256) for c in range(NCORES)], axis=1).astype(f)
    nc_p = np.stack([R[c]["oconv"].transpose(0, 3, 1, 2).reshape(2, 2, 256) for c in range(NCORES)], axis=1).astype(f)
    nk_s = np.concatenate([R[c]["okTs"].transpose(0, 2, 1).reshape(2, SBATCH, SS, NH, 128) for c in range(NCORES)], axis=1).astype(f)
    nv_s = np.concatenate([R[c]["ovs"].reshape(2, SBATCH, SS, NH, 128) for c in range(NCORES)], axis=1).astype(f)
    np_s = np.concatenate([R[c]["opools"].transpose(0, 1, 4, 2, 3).reshape(2, SBATCH, 15, 256) for c in range(NCORES)], axis=1).astype(f)
    nc_s = np.concatenate([R[c]["oconvs"].transpose(0, 1, 4, 2, 3).reshape(2, SBATCH, 2, 256) for c in range(NCORES)], axis=1).astype(f)
    return (np.ascontiguousarray(y_prompt), np.ascontiguousarray(y_sample), np.ascontiguousarray(nk_p),
            np.ascontiguousarray(nv_p), np.ascontiguousarray(np_p), np.ascontiguousarray(nc_p),
            np.ascontiguousarray(nk_s), np.ascontiguousarray(nv_s), np.ascontiguousarray(np_s),
            np.ascontiguousarray(nc_s))
```
